# Optimizing a Trainium2 kernel written in Bass

```python
import math
import jax, jax.numpy as jnp
from jax import lax
import numpy as np

D_MODEL = 2048
BATCH = 4
SEQ = 4096
DEPTH = 1

CHUNK = 64
LEFT_CHUNKS = 8
BAND = (LEFT_CHUNKS + 1) * CHUNK
MEM_LEN = 256
A_HEADS = 16
A_HEAD_DIM = 64
A_WIDTH = A_HEADS * A_HEAD_DIM
MAX_REL = 128
B_HEADS = 16
QK_NOPE = 128
QK_ROPE = 64
V_HEAD = 128
Q_LORA = 512
KV_LORA = 512
ROPE_THETA = 10000.0
B_WIDTH = B_HEADS * V_HEAD
Q_BLOCK = 128
X_HEADS = 4
X_HEAD_DIM = D_MODEL // X_HEADS
D_FF = 5504
N_BRANCH = 2
IN_SPLITS = (A_WIDTH, 2 * A_WIDTH, 3 * A_WIDTH, 3 * A_WIDTH + Q_LORA,
             3 * A_WIDTH + Q_LORA + KV_LORA + QK_ROPE)
IN_COLS = 3 * A_WIDTH + Q_LORA + KV_LORA + QK_ROPE + N_BRANCH * D_MODEL
ALPHA = (2.0 * DEPTH) ** 0.25
BETA = (8.0 * DEPTH) ** -0.25
LN_EPS = 1e-5
RMS_EPS = 1e-6
NEG_INF = -1e30

kernel_name = 'hybrid_chunked_mla_macaron_deepnorm'


def layer_norm(x, g, b):
    xf = x.astype(jnp.float32)
    mu = xf.mean(-1, keepdims=True)
    var = jnp.square(xf - mu).mean(-1, keepdims=True)
    y = (xf - mu) * lax.rsqrt(var + LN_EPS) * g.astype(jnp.float32) + b.astype(jnp.float32)
    return y.astype(x.dtype)


def rms_norm(x, g):
    xf = x.astype(jnp.float32)
    y = xf * lax.rsqrt(jnp.square(xf).mean(-1, keepdims=True) + RMS_EPS) * g.astype(jnp.float32)
    return y.astype(x.dtype)


def swiglu_ffn(x, w_in, w_out):
    g, u = jnp.split(x @ w_in, 2, axis=-1)
    return (jax.nn.silu(g) * u) @ w_out


def rope_tables(seq_len, dim):
    inv = 1.0 / (ROPE_THETA ** (jnp.arange(0, dim, 2, dtype=jnp.float32) / dim))
    ang = jnp.arange(seq_len, dtype=jnp.float32)[:, None] * inv[None, :]
    return jnp.cos(ang)[:, None, :], jnp.sin(ang)[:, None, :]


def apply_rope(x, cos, sin):
    x1, x2 = jnp.split(x.astype(jnp.float32), 2, axis=-1)
    return jnp.concatenate([x1 * cos - x2 * sin, x2 * cos + x1 * sin], axis=-1).astype(x.dtype)


def chunked_relbias_attention(q, k, v, rel_bias):
    b, s, h, d = q.shape
    nc = s // CHUNK
    qc = q.reshape(b, nc, CHUNK, h, d)

    def band(t):
        tc = t.reshape(b, nc, CHUNK, h, d)
        tp = jnp.pad(tc, ((0, 0), (LEFT_CHUNKS, 0), (0, 0), (0, 0), (0, 0)))
        return jnp.concatenate([tp[:, j:j + nc] for j in range(LEFT_CHUNKS + 1)], axis=2)

    kb, vb = band(k), band(v)
    scores = jnp.einsum('bcqhd,bckhd->bchqk', qc, kb).astype(jnp.float32) / math.sqrt(d)
    qi = jnp.arange(CHUNK)[:, None]
    kj = jnp.arange(BAND)[None, :]
    dist = LEFT_CHUNKS * CHUNK + qi - kj
    idx = jnp.clip(dist, -MAX_REL, MAX_REL) + MAX_REL
    bias = rel_bias.astype(jnp.float32)[:, idx]
    key_chunk = jnp.arange(nc)[:, None] - LEFT_CHUNKS + kj // CHUNK
    valid = key_chunk >= 0
    scores = jnp.where(valid[None, :, None, None, :], scores + bias[None, None], NEG_INF)
    p = jax.nn.softmax(scores, axis=-1).astype(v.dtype)
    o = jnp.einsum('bchqk,bckhd->bcqhd', p, vb)
    return o.reshape(b, s, h * d)


def mla_attention(q_lat, kv_lat, q_a_norm, w_q_b, kv_a_norm, w_kv_b, cos, sin):
    b, s, _ = q_lat.shape
    c_q = rms_norm(q_lat, q_a_norm)
    q = (c_q @ w_q_b).reshape(b, s, B_HEADS, QK_NOPE + QK_ROPE)
    q_nope, q_pe = jnp.split(q, [QK_NOPE], axis=-1)
    q = jnp.concatenate([q_nope, apply_rope(q_pe, cos, sin)], axis=-1)
    c_kv, k_pe = jnp.split(kv_lat, [KV_LORA], axis=-1)
    c_kv = rms_norm(c_kv, kv_a_norm)
    kv = (c_kv @ w_kv_b).reshape(b, s, B_HEADS, QK_NOPE + V_HEAD)
    k_nope, v = jnp.split(kv, [QK_NOPE], axis=-1)
    k_pe = apply_rope(k_pe[:, :, None, :], cos, sin)
    k = jnp.concatenate([k_nope, jnp.broadcast_to(k_pe, (b, s, B_HEADS, QK_ROPE))], axis=-1)
    scale = (QK_NOPE + QK_ROPE) ** -0.5
    nb = s // Q_BLOCK
    q_blocks = jnp.moveaxis(q.reshape(b, nb, Q_BLOCK, B_HEADS, QK_NOPE + QK_ROPE), 1, 0)
    key_chunk = jnp.arange(s) // CHUNK

    def block(args):
        qb, start = args
        sc = jnp.einsum('bqhd,bkhd->bhqk', qb, k).astype(jnp.float32) * scale
        q_chunk = (start + jnp.arange(Q_BLOCK)) // CHUNK
        allowed = key_chunk[None, :] <= q_chunk[:, None]
        sc = jnp.where(allowed[None, None], sc, NEG_INF)
        p = jax.nn.softmax(sc, axis=-1).astype(v.dtype)
        return jnp.einsum('bhqk,bkhd->bqhd', p, v)

    starts = jnp.arange(nb, dtype=jnp.int32) * Q_BLOCK
    o = lax.map(block, (q_blocks, starts))
    return jnp.moveaxis(o, 0, 1).reshape(b, s, B_WIDTH)


def hybrid_mixer(u, w_in, gate_bias, rel_bias, q_a_norm, w_q_b, kv_a_norm, w_kv_b,
                 w_o_a, w_o_b, w_out, cos, sin):
    b, s, _ = u.shape
    proj = u @ w_in
    qa, ka, va, q_lat, kv_lat, gates = jnp.split(proj, list(IN_SPLITS), axis=-1)
    shp = (b, s, A_HEADS, A_HEAD_DIM)
    y_a = chunked_relbias_attention(qa.reshape(shp), ka.reshape(shp), va.reshape(shp), rel_bias) @ w_o_a
    y_b = mla_attention(q_lat, kv_lat, q_a_norm, w_q_b, kv_a_norm, w_kv_b, cos, sin) @ w_o_b
    g = jax.nn.sigmoid((gates + gate_bias).astype(jnp.float32)).astype(u.dtype)
    g_a, g_b = jnp.split(g, 2, axis=-1)
    return (g_a * y_a + g_b * y_b) @ w_out


def memory_cross_attention(h, mem, mem_ln_g, mem_ln_b, w_xq, w_xkv, w_xo):
    b, s, _ = h.shape
    m_len = mem.shape[1]
    m = layer_norm(mem, mem_ln_g, mem_ln_b)
    q = (h @ w_xq).reshape(b, s, X_HEADS, X_HEAD_DIM)
    kv = (m @ w_xkv).reshape(b, m_len, 2, X_HEADS, X_HEAD_DIM)
    k, v = kv[:, :, 0], kv[:, :, 1]
    sc = jnp.einsum('bqhd,bkhd->bhqk', q, k).astype(jnp.float32) / math.sqrt(X_HEAD_DIM)
    p = jax.nn.softmax(sc, axis=-1).astype(v.dtype)
    o = jnp.einsum('bhqk,bkhd->bqhd', p, v).reshape(b, s, D_MODEL)
    return o @ w_xo


def setup_inputs(seed: int = 0) -> dict:
    key = jax.random.key(seed)
    ks = iter(jax.random.split(key, 40))

    def nrm(shape, scale):
        return jax.random.normal(next(ks), shape, jnp.float32) * scale

    def gain(shape):
        return 1.0 + nrm(shape, 0.02)

    L, D = DEPTH, D_MODEL
    x = nrm((BATCH, SEQ, D), 1.0)
    mem = nrm((BATCH, MEM_LEN, D), 1.0)
    ffn1_w_in = nrm((L, D, 2 * D_FF), D ** -0.5)
    ffn1_w_out = nrm((L, D_FF, D), BETA * D_FF ** -0.5)
    ln_ffn1_g = gain((L, D))
    ln_ffn1_b = nrm((L, D), 0.02)
    w_in = jnp.concatenate([
        nrm((L, D, 2 * A_WIDTH), D ** -0.5),
        nrm((L, D, A_WIDTH), BETA * D ** -0.5),
        nrm((L, D, Q_LORA + KV_LORA + QK_ROPE), D ** -0.5),
        nrm((L, D, N_BRANCH * D), D ** -0.5),
    ], axis=-1)
    gate_bias = nrm((L, N_BRANCH * D), 0.02)
    rel_bias = nrm((L, A_HEADS, 2 * MAX_REL + 1), 0.3)
    q_a_norm = gain((L, Q_LORA))
    w_q_b = nrm((L, Q_LORA, B_HEADS * (QK_NOPE + QK_ROPE)), Q_LORA ** -0.5)
    kv_a_norm = gain((L, KV_LORA))
    kv_scale = jnp.concatenate([jnp.ones((QK_NOPE,), jnp.float32), jnp.full((V_HEAD,), BETA, jnp.float32)])
    w_kv_b = (nrm((L, KV_LORA, B_HEADS, QK_NOPE + V_HEAD), KV_LORA ** -0.5) * kv_scale
              ).reshape(L, KV_LORA, B_HEADS * (QK_NOPE + V_HEAD))
    w_o_a = nrm((L, A_WIDTH, D), BETA * A_WIDTH ** -0.5)
    w_o_b = nrm((L, B_WIDTH, D), BETA * B_WIDTH ** -0.5)
    w_out = nrm((L, D, D), BETA * D ** -0.5)
    ln_mix_g = gain((L, D))
    ln_mix_b = nrm((L, D), 0.02)
    mem_ln_g = gain((L, D))
    mem_ln_b = nrm((L, D), 0.02)
    w_xq = nrm((L, D, D), D ** -0.5)
    xkv_scale = jnp.array([1.0, BETA], jnp.float32)[:, None, None]
    w_xkv = (nrm((L, D, 2, X_HEADS, X_HEAD_DIM), D ** -0.5) * xkv_scale).reshape(L, D, 2 * D)
    w_xo = nrm((L, D, D), BETA * D ** -0.5)
    ln_x_g = gain((L, D))
    ln_x_b = nrm((L, D), 0.02)
    ffn2_w_in = nrm((L, D, 2 * D_FF), D ** -0.5)
    ffn2_w_out = nrm((L, D_FF, D), BETA * D_FF ** -0.5)
    ln_ffn2_g = gain((L, D))
    ln_ffn2_b = nrm((L, D), 0.02)
    return {'x': x, 'mem': mem,
            'ffn1_w_in': ffn1_w_in, 'ffn1_w_out': ffn1_w_out, 'ln_ffn1_g': ln_ffn1_g, 'ln_ffn1_b': ln_ffn1_b,
            'w_in': w_in, 'gate_bias': gate_bias, 'rel_bias': rel_bias,
            'q_a_norm': q_a_norm, 'w_q_b': w_q_b, 'kv_a_norm': kv_a_norm, 'w_kv_b': w_kv_b,
            'w_o_a': w_o_a, 'w_o_b': w_o_b, 'w_out': w_out, 'ln_mix_g': ln_mix_g, 'ln_mix_b': ln_mix_b,
            'mem_ln_g': mem_ln_g, 'mem_ln_b': mem_ln_b, 'w_xq': w_xq, 'w_xkv': w_xkv, 'w_xo': w_xo,
            'ln_x_g': ln_x_g, 'ln_x_b': ln_x_b,
            'ffn2_w_in': ffn2_w_in, 'ffn2_w_out': ffn2_w_out, 'ln_ffn2_g': ln_ffn2_g, 'ln_ffn2_b': ln_ffn2_b}


def reference(x, mem, ffn1_w_in, ffn1_w_out, ln_ffn1_g, ln_ffn1_b,
              w_in, gate_bias, rel_bias, q_a_norm, w_q_b, kv_a_norm, w_kv_b,
              w_o_a, w_o_b, w_out, ln_mix_g, ln_mix_b,
              mem_ln_g, mem_ln_b, w_xq, w_xkv, w_xo, ln_x_g, ln_x_b,
              ffn2_w_in, ffn2_w_out, ln_ffn2_g, ln_ffn2_b):
    cos, sin = rope_tables(x.shape[1], QK_ROPE)
    h = x
    for l in range(DEPTH):
        h = layer_norm(ALPHA * h + 0.5 * swiglu_ffn(h, ffn1_w_in[l], ffn1_w_out[l]), ln_ffn1_g[l], ln_ffn1_b[l])
        mix = hybrid_mixer(h, w_in[l], gate_bias[l], rel_bias[l], q_a_norm[l], w_q_b[l], kv_a_norm[l],
                           w_kv_b[l], w_o_a[l], w_o_b[l], w_out[l], cos, sin)
        h = layer_norm(ALPHA * h + mix, ln_mix_g[l], ln_mix_b[l])
        xa = memory_cross_attention(h, mem, mem_ln_g[l], mem_ln_b[l], w_xq[l], w_xkv[l], w_xo[l])
        h = layer_norm(ALPHA * h + xa, ln_x_g[l], ln_x_b[l])
        h = layer_norm(ALPHA * h + 0.5 * swiglu_ffn(h, ffn2_w_in[l], ffn2_w_out[l]), ln_ffn2_g[l], ln_ffn2_b[l])
    return h
```

```python
import contextlib
import numpy as np
import ml_dtypes
import concourse.bass as bass
import concourse.mybir as mybir
from concourse.bass_utils import run_bass_kernel_spmd

F32 = mybir.dt.float32
BF16 = mybir.dt.bfloat16
AF = mybir.ActivationFunctionType
ALU = mybir.AluOpType

D = 2048
NDC = 16
SEQ = 4096
HALF = 2048
DFF = 5504
NFC = 43
ALPHA = 2.0 ** 0.25
C_FFN = 0.5 / ALPHA
C_ONE = 1.0 / ALPHA
NEG = -30000.0

PC = {}
_o = 0
for _n, _w in [("ln1_g", 16), ("ln1_b", 16), ("ln2_g", 16), ("ln2_b", 16), ("ln3_g", 16), ("ln3_b", 16),
               ("ln4_g", 16), ("ln4_b", 16), ("mln_g", 16), ("mln_b", 16), ("gate_b", 32), ("qan", 4),
               ("kvan", 4), ("oth", 1), ("eps_dn", 1), ("eps_ln", 1), ("eps_rms", 1), ("zero", 1)]:
    PC[_n] = _o
    _o += _w
NPRM = _o

ENGS = ("pe", "act", "dve", "pool", "sp")


class Prog:
    def __init__(self, nc, stack):
        self.nc = nc
        self.stack = stack
        self.ops = {e: [] for e in ENGS}
        self.count = {e: 0 for e in ENGS}
        self.semname = {e: "S_" + e for e in ENGS}
        self.handles = {}
        self.dma_sems = {}
        self.seen = {e: {} for e in ENGS}
        self.last_write = {}
        self.readers = {}
        self.final_waits = []
        for e in ENGS:
            self._sem(self.semname[e])

    def _sem(self, name):
        if name not in self.handles:
            self.handles[name] = self.stack.enter_context(self.nc.semaphore(name))
        return self.handles[name]

    def _need(self, eng, dep, waits):
        sem, val, deng = dep
        if deng == eng:
            return
        if self.seen[eng].get(sem, 0) >= val:
            return
        waits[sem] = max(waits.get(sem, 0), val)

    def _deps(self, eng, reads, writes):
        waits = {}
        for b in reads:
            lw = self.last_write.get(b)
            if lw is not None:
                self._need(eng, lw, waits)
        for b in writes:
            lw = self.last_write.get(b)
            if lw is not None:
                self._need(eng, lw, waits)
            for r in self.readers.get(b, ()):
                self._need(eng, r, waits)
        for sem, val in waits.items():
            self.seen[eng][sem] = val
        return list(waits.items())

    def _commit(self, token, reads, writes):
        for b in reads:
            self.readers.setdefault(b, []).append(token)
        for b in writes:
            self.last_write[b] = token
            self.readers[b] = []

    def op(self, eng, fn, reads=(), writes=(), inc=True):
        waits = self._deps(eng, reads, writes)
        if inc:
            self.count[eng] += 1
            token = (self.semname[eng], self.count[eng], eng)
            self.ops[eng].append((waits, fn, (self.semname[eng], 1)))
        else:
            token = (self.semname[eng], self.count[eng] + 1, eng)
            self.ops[eng].append((waits, fn, None))
        self._commit(token, reads, writes)
        return token

    def dma(self, queue, semname, fn, reads=(), writes=(), final=False):
        self._sem(semname)
        waits = self._deps(queue, reads, writes)
        prev = self.dma_sems.get(semname, 0)
        if prev and self.seen[queue].get(semname, 0) < prev:
            waits.append((semname, prev))
            self.seen[queue][semname] = prev
        self.dma_sems[semname] = prev + 16
        val = prev + 16
        token = (semname, val, "dma")
        self.ops[queue].append((waits, fn, (semname, 16)))
        self._commit(token, reads, writes)
        if final:
            self.final_waits.append((semname, val))
        return token

    def end_phase(self, last=False):
        toks = [(self.semname[e], self.count[e]) for e in ENGS if self.count[e]]
        toks += [(s, v) for s, v in self.dma_sems.items()]
        for e in ENGS:
            waits = []
            for s, v in toks:
                if s == self.semname[e]:
                    continue
                if self.seen[e].get(s, 0) < v:
                    waits.append((s, v))
                    self.seen[e][s] = v
            if waits:
                self.ops[e].append((waits, None, None))
        H = self.handles
        ops = self.ops

        def runner(e):
            def run(engine):
                for waits, fn, inc in ops[e]:
                    for s, v in waits:
                        engine.wait_ge(H[s], v)
                    if fn is None:
                        continue
                    ins = fn(engine)
                    if inc is not None:
                        ins.then_inc(H[inc[0]], inc[1])
            return run

        with self.nc.Block() as block:
            block.tensor(runner("pe"))
            block.scalar(runner("act"))
            block.vector(runner("dve"))
            block.gpsimd(runner("pool"))
            block.sync(runner("sp"))
        self.ops = {e: [] for e in ENGS}
        self.last_write = {}
        self.readers = {}


class Ctx:
    pass


def _cols(ap2d, c0, n):
    return ap2d[:, c0:c0 + n]


def load_w(K, W, r0, nk, c0, ncols, extra=None):
    P = K.P
    i = K.ring_i % K.NRING
    K.ring_i += 1
    slot = K.ring[i]
    key = f"wr{i}"
    view = slot[:, 0:nk * ncols].rearrange("p (k f) -> p k f", k=nk)
    src = W[r0:r0 + nk * 128, c0:c0 + ncols].rearrange("(k p) f -> p k f", p=128)
    P.dma("pool", f"D_wr{i}", lambda e: e.dma_start(out=view, in_=src), writes=[key])
    return view, key


def mm_group(K, out_ap, pairs, reads, wkey):
    n = len(pairs)
    for i, (l, r) in enumerate(pairs):
        K.P.op("pe", (lambda e, l=l, r=r, i=i: e.matmul(out_ap, l, r, start=(i == 0), stop=(i == n - 1))),
               reads=reads, writes=[wkey], inc=(i == n - 1))


def ln_stats_and_norm(K, z, zkey, nt, gcol, bcol, epscol, after_chunk, nchunks=NDC, dtot=D, rms=False,
                      banks=(0, 1, 2, 3), defer=False):
    P = K.P
    T = K.lnt
    acc1, acc2, sq = T["acc1"], T["acc2"], T["sq"]
    prm = K.prm
    zk = [zkey + str(c) for c in range(nchunks)]
    nh = (nt + 511) // 512
    hw = min(nt, 512)
    for c in range(nchunks):
        sqb = T["t"][c % 2]
        sqk = f"lnt{c % 2}"
        if c == 0:
            P.op("act", lambda e: e.activation(out=acc2[:, 0:nt], in_=z[:, 0, :], func=AF.Square), reads=[zk[0]], writes=["acc2"])
        else:
            P.op("act", lambda e, c=c, sqb=sqb: e.activation(out=sqb[:, 0:nt], in_=z[:, c, :], func=AF.Square), reads=[zk[c]], writes=[sqk])
            P.op("dve", lambda e, sqb=sqb: e.tensor_tensor(out=acc2[:, 0:nt], in0=acc2[:, 0:nt], in1=sqb[:, 0:nt], op=ALU.add),
                 reads=[sqk, "acc2"], writes=["acc2"])
        if not rms:
            if c == 1:
                P.op("dve", lambda e: e.tensor_tensor(out=acc1[:, 0:nt], in0=z[:, 0, :], in1=z[:, 1, :], op=ALU.add),
                     reads=zk[0:2], writes=["acc1"])
            elif c >= 2:
                P.op("dve", lambda e, c=c: e.tensor_tensor(out=acc1[:, 0:nt], in0=acc1[:, 0:nt], in1=z[:, c, :], op=ALU.add),
                     reads=[zk[c], "acc1"], writes=["acc1"])

    def part_b():
        _ln_part_b(K, z, zk, nt, gcol, bcol, epscol, after_chunk, nchunks, dtot, rms, banks, nh, hw)
    if defer:
        return part_b
    part_b()


def _ln_part_b(K, z, zk, nt, gcol, bcol, epscol, after_chunk, nchunks, dtot, rms, banks, nh, hw):
    P = K.P
    T = K.lnt
    acc1, acc2, sq = T["acc1"], T["acc2"], T["sq"]
    prm = K.prm
    pb = K.pb
    for h in range(nh):
        sl = slice(h * hw, (h + 1) * hw)
        if not rms:
            b1 = banks[(2 * h) % len(banks)]
            P.op("pe", lambda e, sl=sl, b1=b1: e.matmul(pb[b1][:, 0:hw], K.ones_f[:], acc1[:, sl], start=True, stop=True),
                 reads=["acc1", "ones_f"], writes=[f"pb{b1}"])
        b2 = banks[(2 * h + 1) % len(banks)]
        P.op("pe", lambda e, sl=sl, b2=b2: e.matmul(pb[b2][:, 0:hw], K.ones_f[:], acc2[:, sl], start=True, stop=True),
             reads=["acc2", "ones_f"], writes=[f"pb{b2}"])
        if not rms:
            P.op("dve", lambda e, sl=sl, b1=b1: e.tensor_scalar(out=acc1[:, sl], in0=pb[b1][:, 0:hw], scalar1=1.0 / dtot, scalar2=None, op0=ALU.mult),
                 reads=[f"pb{b1}"], writes=["acc1"])
            P.op("dve", lambda e, sl=sl: e.tensor_tensor(out=sq[:, sl], in0=acc1[:, sl], in1=acc1[:, sl], op=ALU.mult),
                 reads=["acc1"], writes=["sq"])
            P.op("dve", lambda e, sl=sl, b2=b2: e.scalar_tensor_tensor(out=acc2[:, sl], in0=pb[b2][:, 0:hw], scalar=1.0 / dtot, in1=sq[:, sl], op0=ALU.mult, op1=ALU.subtract),
                 reads=[f"pb{b2}", "sq"], writes=["acc2"])
            P.op("act", lambda e, sl=sl: e.activation(out=acc2[:, sl], in_=acc2[:, sl], func=AF.Sqrt, bias=prm[:, epscol:epscol + 1], scale=1.0),
                 reads=["acc2", "prm"], writes=["acc2"])
        else:
            P.op("act", lambda e, sl=sl, b2=b2: e.activation(out=acc2[:, sl], in_=pb[b2][:, 0:hw], func=AF.Sqrt, bias=prm[:, epscol:epscol + 1], scale=1.0 / dtot),
                 reads=[f"pb{b2}", "prm"], writes=["acc2"])
        P.op("dve", lambda e, sl=sl: e.reciprocal(out=acc2[:, sl], in_=acc2[:, sl]), reads=["acc2"], writes=["acc2"])
        if not rms:
            P.op("dve", lambda e, sl=sl: e.scalar_tensor_tensor(out=sq[:, sl], in0=acc1[:, sl], scalar=-1.0, in1=acc2[:, sl], op0=ALU.mult, op1=ALU.mult),
                 reads=["acc1", "acc2"], writes=["sq"])
    for c in range(nchunks):
        tb = T["t"][c % 2]
        tk = f"lnt{c % 2}"
        P.op("dve", lambda e, c=c, tb=tb: e.tensor_tensor(out=tb[:, 0:nt], in0=z[:, c, :], in1=acc2[:, 0:nt], op=ALU.mult),
             reads=[zk[c], "acc2"], writes=[tk])
        if not rms:
            P.op("dve", lambda e, tb=tb: e.tensor_tensor(out=tb[:, 0:nt], in0=tb[:, 0:nt], in1=sq[:, 0:nt], op=ALU.add),
                 reads=[tk, "sq"], writes=[tk])
            P.op("act", lambda e, c=c, tb=tb: e.activation(out=z[:, c, :], in_=tb[:, 0:nt], func=AF.Identity,
                                                          bias=prm[:, bcol + c:bcol + c + 1], scale=prm[:, gcol + c:gcol + c + 1]),
                 reads=[tk, "prm"], writes=[zk[c]])
        else:
            P.op("act", lambda e, c=c, tb=tb: e.activation(out=z[:, c, :], in_=tb[:, 0:nt], func=AF.Identity,
                                                          bias=prm[:, PC["zero"]:PC["zero"] + 1], scale=prm[:, gcol + c:gcol + c + 1]),
                 reads=[tk, "prm"], writes=[zk[c]])
        after_chunk(c)


FFN_PIECES = [(0, 8), (8, 16), (16, 24), (24, 32), (32, 40), (40, 43)]


def ffn_phase(K, tiles, x_f32, x_bsrc, bsrc_is_f32, w_in, w_out, gcol, bcol, out_f32, out_b16, TT=1024):
    nc, P = K.nc, K.P
    K.uid = getattr(K, "uid", 0) + 1
    with contextlib.ExitStack() as st:
        sb = lambda n, s, d=F32: st.enter_context(nc.sbuf_tensor(f"u{K.uid}_" + n, s, d))
        z = sb("f_z", [128, NDC, TT])
        xb = sb("f_xb", [128, NDC, TT], BF16)
        at = sb("f_at", [128, 8, TT], BF16)
        xs = [sb(f"f_xs{i}", [128, TT]) for i in range(2)]
        sg = [sb(f"f_sg{i}", [128, 512]) for i in range(2)]
        hbs = [sb(f"f_hb{i}", [128, TT], BF16) for i in range(2)]
        K.lnt = {"acc1": sb("f_acc1", [128, TT]), "acc2": sb("f_acc2", [128, TT]), "sq": sb("f_sq", [128, TT]),
                 "t": [sb(f"f_t{i}", [128, TT]) for i in range(2)]}
        nh = TT // 512
        pb = K.pb
        gi = 0
        wo_i = 0
        xs_i = 0
        hb_i = 0
        def load_xb(t0):
            src = x_bsrc[:, t0:t0 + TT].rearrange("(c p) t -> p c t", p=128)
            q = "pool" if bsrc_is_f32 else "sp"
            P.dma(q, "D_fxb", lambda e, src=src: e.dma_start(out=xb[:], in_=src), writes=["f_xb"])
        pending = []
        load_xb(tiles[0][0])
        for ti, (t0, of32, ob16) in enumerate(tiles):
            for pi, (f0, f1) in enumerate(FFN_PIECES):
                npf = f1 - f0
                fj = f0
                while fj < f1:
                    ns = min(4, f1 - fj)
                    gv, gk = load_w(K, w_in, 0, NDC, fj * 128, ns * 128)
                    uv, uk = load_w(K, w_in, 0, NDC, DFF + fj * 128, ns * 128)
                    for j in range(ns):
                        fl = fj + j - f0
                        for h in range(nh):
                            ba, bb = 2 * (gi % 3), 2 * (gi % 3) + 1
                            gi += 1
                            cs = slice(h * 512, (h + 1) * 512)
                            mm_group(K, pb[ba][:], [(gv[:, kc, j * 128:(j + 1) * 128], xb[:, kc, cs]) for kc in range(NDC)],
                                     reads=[gk, "f_xb"], wkey=f"pb{ba}")
                            mm_group(K, pb[bb][:], [(uv[:, kc, j * 128:(j + 1) * 128], xb[:, kc, cs]) for kc in range(NDC)],
                                     reads=[uk, "f_xb"], wkey=f"pb{bb}")
                            s = sg[gi % 2]
                            sk = f"f_sg{gi % 2}"
                            P.op("act", lambda e, s=s, ba=ba: e.activation(out=s[:], in_=pb[ba][:], func=AF.Silu),
                                 reads=[f"pb{ba}"], writes=[sk])
                            P.op("dve", lambda e, s=s, bb=bb, fl=fl, cs=cs: e.tensor_tensor(out=at[:, fl, cs], in0=s[:], in1=pb[bb][:], op=ALU.mult),
                                 reads=[sk, f"pb{bb}"], writes=[f"f_at{fl}"])
                    fj += ns
                    if pi == 0 and pending:
                        pending.pop(0)()
                if pi == len(FFN_PIECES) - 1 and ti + 1 < len(tiles):
                    load_xb(tiles[ti + 1][0])
                for cs4 in range(4):
                    wv, wk = load_w(K, w_out, f0 * 128, npf, cs4 * 512, 512)
                    for dl in range(4):
                        dc = cs4 * 4 + dl
                        if pi == 0:
                            xv = xs[xs_i % 2]
                            xk = f"f_xs{xs_i % 2}"
                            xs_i += 1
                            xsrc = x_f32[dc * 128:(dc + 1) * 128, t0:t0 + TT]
                            P.dma("sp", "D_" + xk, lambda e, xv=xv, xsrc=xsrc: e.dma_start(out=xv[:], in_=xsrc), writes=[xk])
                        for h in range(nh):
                            bo = 6 + (wo_i % 2)
                            wo_i += 1
                            cs = slice(h * 512, (h + 1) * 512)
                            mm_group(K, pb[bo][:], [(wv[:, k, dl * 128:(dl + 1) * 128], at[:, k, cs]) for k in range(npf)],
                                     reads=[wk] + [f"f_at{k}" for k in range(npf)], wkey=f"pb{bo}")
                            if pi == 0:
                                P.op("dve", lambda e, bo=bo, dc=dc, cs=cs, xv=xv: e.scalar_tensor_tensor(
                                    out=z[:, dc, cs], in0=pb[bo][:], scalar=C_FFN, in1=xv[:, cs], op0=ALU.mult, op1=ALU.add),
                                    reads=[f"pb{bo}", xk], writes=[f"f_z{dc}"])
                            else:
                                P.op("dve", lambda e, bo=bo, dc=dc, cs=cs: e.scalar_tensor_tensor(
                                    out=z[:, dc, cs], in0=pb[bo][:], scalar=C_FFN, in1=z[:, dc, cs], op0=ALU.mult, op1=ALU.add),
                                    reads=[f"pb{bo}", f"f_z{dc}"], writes=[f"f_z{dc}"])

            def after(c, ob16=ob16):
                nonlocal hb_i
                if ob16 is not None:
                    hv = hbs[hb_i % 2]
                    hk = f"f_hb{hb_i % 2}"
                    hb_i += 1
                    P.op("dve", lambda e, hv=hv, c=c: e.tensor_copy(out=hv[:], in_=z[:, c, :]), reads=[f"f_z{c}"], writes=[hk])
                    dst = out_b16[c * 128:(c + 1) * 128, ob16:ob16 + TT]
                    P.dma("sp", "D_" + hk, lambda e, hv=hv, dst=dst: e.dma_start(out=dst, in_=hv[:]), reads=[hk])
            pb_ = ln_stats_and_norm(K, z, "f_z", TT, gcol, bcol, PC["eps_dn"], after, defer=True)

            def tail(pb_=pb_, of32=of32):
                pb_()
                if of32 is not None:
                    dst = out_f32[:, of32:of32 + TT].rearrange("(c p) t -> p c t", p=128)
                    P.dma("sp", "D_fzo", lambda e, dst=dst: e.dma_start(out=dst, in_=z[:]), reads=[f"f_z{c}" for c in range(NDC)],
                          final=True)
            pending.append(tail)
        while pending:
            pending.pop(0)()
        P.end_phase()


def nb(K, banks):
    b = banks[K.bi % len(banks)]
    K.bi += 1
    return b


def evac_copy(K, out_ap, bank_ap, bkey, wkey):
    K.ev += 1
    if K.ev % 2:
        K.P.op("act", lambda e: e.copy(out=out_ap, in_=bank_ap), reads=[bkey], writes=[wkey])
    else:
        K.P.op("dve", lambda e: e.tensor_copy(out=out_ap, in_=bank_ap), reads=[bkey], writes=[wkey])


def fm_linear(K, W, c0, nch, act, akey, ttiles, evac, nk=NDC, banks=(0, 1, 2, 3), tw=512):
    j = 0
    while j < nch:
        ns = min(4, nch - j)
        wv, wk = load_w(K, W, 0, nk, c0 + j * 128, ns * 128)
        for jj in range(ns):
            for tt in ttiles:
                b = nb(K, banks)
                mm_group(K, K.pb[b][:, 0:tw], [(wv[:, kc, jj * 128:(jj + 1) * 128], act[:, kc, tt * tw:(tt + 1) * tw]) for kc in range(nk)],
                         reads=[wk, akey], wkey=f"pb{b}")
                evac(b, j + jj, tt)
        j += ns


def proj_phase(K, H1B, w_in, QA, KA, VA, CQ, CKV, KPE, G, rope_d, subs=("own", "ctx"), ntt_override=None):
    nc, P = K.nc, K.P
    pb = K.pb
    prm = K.prm
    with contextlib.ExitStack() as st:
        sb = lambda n, s, d=F32: st.enter_context(nc.sbuf_tensor(n, s, d))
        hb = sb("p_hb", [128, NDC, HALF], BF16)
        ql = sb("p_ql", [128, 4, HALF])
        stb = [sb(f"p_stb{i}", [128, HALF], BF16) for i in range(2)]
        stf = [sb(f"p_stf{i}", [128, HALF]) for i in range(2)]
        vst = [sb(f"p_vst{i}", [128, 512], BF16) for i in range(3)]
        cst = [sb(f"p_cst{i}", [128, 512], BF16) for i in range(3)]
        rope = sb("p_rope", [64, 2, HALF])
        rt = [sb(f"p_rt{i}", [64, 512]) for i in range(3)]
        K.lnt = {"acc1": None, "acc2": sb("p_acc2", [128, 512]), "sq": sb("p_sq", [128, 512]),
                 "t": [sb(f"p_t{i}", [128, 512]) for i in range(2)]}
        cnt = {"stb": 0, "stf": 0, "vst": 0, "cst": 0}

        def staged_fm(c0, nch, ttiles, dst, dst_col0, kind):
            def evac(b, j, tt):
                if tt == ttiles[0]:
                    cnt[kind] += 1
                i = cnt[kind] % 2
                if kind == "stb":
                    sv, sk = stb[i], f"p_stb{i}"
                    evac_copy(K, sv[:, tt * 512:(tt + 1) * 512], pb[b][:], f"pb{b}", sk)
                else:
                    sv, sk = stf[i], f"p_stf{i}"
                    gc = PC["gate_b"] + j
                    P.op("act", lambda e, sv=sv, b=b, tt=tt, gc=gc: e.activation(out=sv[:, tt * 512:(tt + 1) * 512], in_=pb[b][:], func=AF.Sigmoid,
                                                                           bias=prm[:, gc:gc + 1], scale=1.0),
                         reads=[f"pb{b}", "prm"], writes=[sk])
                if tt == ttiles[-1]:
                    t0, t1 = ttiles[0] * 512, (ttiles[-1] + 1) * 512
                    d = dst[j * 128:(j + 1) * 128, dst_col0:dst_col0 + (t1 - t0)]
                    P.dma("sp", "D_" + sk, lambda e, sv=sv, d=d, t0=t0, t1=t1: e.dma_start(out=d, in_=sv[:, t0:t1]), reads=[sk])
            fm_linear(K, w_in, c0, nch, hb, "p_hb", ttiles, evac)

        def latent(c0, ttiles, gcol, dst, dst_col0, with_rope):
            def evac(b, j, tt):
                evac_copy(K, ql[:, j, tt * 512:(tt + 1) * 512], pb[b][:], f"pb{b}", f"p_ql{j}_{tt}")
            fm_linear(K, w_in, c0, 4, hb, "p_hb", ttiles, evac)
            if with_rope:
                i = K.ring_i % K.NRING
                K.ring_i += 1
                slot = K.ring[i]
                rv = slot[:, 0:NDC * 128].rearrange("p (k f) -> p k f", k=NDC)
                kc0 = c0 + 512
                for (o, s0, n) in [(0, kc0, 64), (64, kc0 + 32, 32), (96, kc0, 32)]:
                    srcw = w_in[:, s0:s0 + n].rearrange("(k p) f -> p k f", p=128)
                    P.dma("pool", f"D_wr{i}", lambda e, o=o, n=n, srcw=srcw: e.dma_start(out=rv[:, :, o:o + n], in_=srcw), writes=[f"wr{i}"])
            for tt in ttiles:
                zv = ql[:, :, tt * 512:(tt + 1) * 512]

                def after(c, tt=tt):
                    cnt["cst"] += 1
                    ci = cnt["cst"] % 3
                    P.op("dve", lambda e, c=c, tt=tt, ci=ci: e.tensor_copy(out=cst[ci][:], in_=ql[:, c, tt * 512:(tt + 1) * 512]),
                         reads=[f"p_ql{c}_{tt}"], writes=[f"p_cst{ci}"])
                    d = dst[c * 128:(c + 1) * 128, dst_col0 + tt * 512:dst_col0 + (tt + 1) * 512]
                    P.dma("sp", f"D_p_cst{ci}", lambda e, d=d, ci=ci: e.dma_start(out=d, in_=cst[ci][:]), reads=[f"p_cst{ci}"])
                ln_rms_tile(K, zv, [f"p_ql{c}_{tt}" for c in range(4)], gcol, after)
                if with_rope:
                    bx, bs = nb(K, (4, 5, 6, 7)), nb(K, (4, 5, 6, 7))
                    cs = slice(tt * 512, (tt + 1) * 512)
                    mm_group(K, pb[bx][0:64, :], [(rv[:, kc, 0:64], hb[:, kc, cs]) for kc in range(NDC)], reads=[f"wr{i}", "p_hb"], wkey=f"pb{bx}")
                    mm_group(K, pb[bs][0:64, :], [(rv[:, kc, 64:128], hb[:, kc, cs]) for kc in range(NDC)], reads=[f"wr{i}", "p_hb"], wkey=f"pb{bs}")
                    cnt["vst"] += 1
                    r1, r2 = rt[0], rt[1]
                    P.op("dve", lambda e, bx=bx, cs=cs: e.tensor_tensor(out=r1[:], in0=pb[bx][0:64, :], in1=rope[:, 0, cs], op=ALU.mult),
                         reads=[f"pb{bx}", "p_rope"], writes=["p_rt0"])
                    P.op("dve", lambda e, bs=bs, cs=cs: e.tensor_tensor(out=r2[:], in0=pb[bs][0:64, :], in1=rope[:, 1, cs], op=ALU.mult),
                         reads=[f"pb{bs}", "p_rope"], writes=["p_rt1"])
                    vi = cnt["vst"] % 3
                    P.op("dve", lambda e, vi=vi: e.tensor_tensor(out=vst[vi][0:64, :], in0=r1[:], in1=r2[:], op=ALU.add),
                         reads=["p_rt0", "p_rt1"], writes=[f"p_vst{vi}"])
                    d = KPE[:, dst_col0 + tt * 512:dst_col0 + (tt + 1) * 512]
                    P.dma("sp", f"D_p_vst{vi}", lambda e, d=d, vi=vi: e.dma_start(out=d, in_=vst[vi][0:64, :]), reads=[f"p_vst{vi}"])

        def v_tm(tblocks, dst_row0):
            for s in range(2):
                wv, wk = load_w(K, w_in, 0, NDC, 2048 + s * 512, 512)
                for bi_, tb in enumerate(tblocks):
                    b = nb(K, (0, 1, 2, 3))
                    mm_group(K, pb[b][:], [(hb[:, kc, tb * 128:(tb + 1) * 128], wv[:, kc, :]) for kc in range(NDC)], reads=[wk, "p_hb"], wkey=f"pb{b}")
                    cnt["vst"] += 1
                    vi = cnt["vst"] % 3
                    evac_copy(K, vst[vi][:], pb[b][:], f"pb{b}", f"p_vst{vi}")
                    d = VA[dst_row0 + bi_ * 128:dst_row0 + (bi_ + 1) * 128, s * 512:(s + 1) * 512]
                    P.dma("sp", f"D_p_vst{vi}", lambda e, d=d, vi=vi: e.dma_start(out=d, in_=vst[vi][:]), reads=[f"p_vst{vi}"])

        for sub in subs:
            col0 = HALF if sub == "own" else 0
            src = H1B[:, col0:col0 + HALF].rearrange("(c p) t -> p c t", p=128)
            for hh in range(2):
                P.dma("sp", f"D_phb{hh}", lambda e, hh=hh, src=src: e.dma_start(out=hb[:, hh * 8:(hh + 1) * 8, :], in_=src[:, hh * 8:(hh + 1) * 8, :]), writes=["p_hb"])
            P.dma("sp", "D_prope", lambda e, col0=col0: e.dma_start(out=rope[:], in_=rope_d[:, :, col0:col0 + HALF]), writes=["p_rope"])
            tts = list(range(4)) if ntt_override is None else ntt_override
            if sub == "own":
                staged_fm(0, 8, tts, QA, 0, "stb")
                staged_fm(1024, 8, tts, KA, 512, "stb")
                v_tm([tb for tb in range(16) if tb // 4 in tts], 512)
                latent(3072, tts, PC["qan"], CQ, 0, False)
                latent(3584, tts, PC["kvan"], CKV, HALF, True)
                staged_fm(4160, 32, tts, G, 0, "stf")
            else:
                latent(3584, tts, PC["kvan"], CKV, 0, True)
                if 3 in tts:
                    def evac_k(b, j, tt):
                        if True:
                            cnt["stb"] += 1
                        i = cnt["stb"] % 2
                        evac_copy(K, stb[i][:, 0:512], pb[b][:], f"pb{b}", f"p_stb{i}")
                        d = KA[j * 128:(j + 1) * 128, 0:512]
                        P.dma("sp", f"D_p_stb{i}", lambda e, d=d, i=i: e.dma_start(out=d, in_=stb[i][:, 0:512]), reads=[f"p_stb{i}"])
                    fm_linear(K, w_in, 1024, 8, hb, "p_hb", [3], evac_k)
                    v_tm([12, 13, 14, 15], 0)
        P.end_phase()


def ln_rms_tile(K, zv, zkeys, gcol, after_chunk, dtot=512):
    P = K.P
    T = K.lnt
    acc2, sq = T["acc2"], T["sq"]
    prm = K.prm
    pb = K.pb
    n = len(zkeys)
    P.op("act", lambda e: e.activation(out=acc2[:], in_=zv[:, 0, :], func=AF.Square), reads=[zkeys[0]], writes=["acc2"])
    for c in range(1, n):
        P.op("act", lambda e, c=c: e.activation(out=sq[:], in_=zv[:, c, :], func=AF.Square), reads=[zkeys[c]], writes=["sq"])
        P.op("dve", lambda e: e.tensor_tensor(out=acc2[:], in0=acc2[:], in1=sq[:], op=ALU.add), reads=["sq", "acc2"], writes=["acc2"])
    b2 = nb(K, (4, 5, 6, 7))
    P.op("pe", lambda e: e.matmul(pb[b2][:], K.ones_f[:], acc2[:], start=True, stop=True), reads=["acc2", "ones_f"], writes=[f"pb{b2}"])
    ec = PC["eps_rms"]
    P.op("act", lambda e: e.activation(out=acc2[:], in_=pb[b2][:], func=AF.Sqrt, bias=prm[:, ec:ec + 1], scale=1.0 / dtot),
         reads=[f"pb{b2}", "prm"], writes=["acc2"])
    P.op("dve", lambda e: e.reciprocal(out=acc2[:], in_=acc2[:]), reads=["acc2"], writes=["acc2"])
    for c in range(n):
        tb = T["t"][c % 2]
        tk = f"lnt{c % 2}"
        P.op("dve", lambda e, c=c, tb=tb: e.tensor_tensor(out=tb[:], in0=zv[:, c, :], in1=acc2[:], op=ALU.mult), reads=[zkeys[c], "acc2"], writes=[tk])
        P.op("act", lambda e, c=c, tb=tb: e.activation(out=zv[:, c, :], in_=tb[:], func=AF.Identity, bias=prm[:, PC["zero"]:PC["zero"] + 1],
                                                      scale=prm[:, gcol + c:gcol + c + 1]), reads=[tk, "prm"], writes=[zkeys[c]])
        after_chunk(c)


def attnA_phase(K, QA, KA, VA, RVR, MA_d, oA, heads=range(16), qts=range(4)):
    nc, P = K.nc, K.P
    pb = K.pb
    prm = K.prm
    with contextlib.ExitStack() as st:
        sb = lambda n, s, d=F32: st.enter_context(nc.sbuf_tensor(n, s, d))
        MA = sb("a_MA", [128, 8, 512])
        q2 = [sb(f"a_q{i}", [128, 2, HALF], BF16) for i in range(2)]
        k2 = [sb(f"a_k{i}", [128, 2560], BF16) for i in range(2)]
        vraw = [sb(f"a_vr{i}", [128, 20, 128], BF16) for i in range(2)]
        vaug = [sb(f"a_va{i}", [128, 20, 2, 128], BF16) for i in range(2)]
        hk = [sb(f"a_hk{i}", [128, 512]) for i in range(2)]
        TB = [sb(f"a_TB{i}", [128, 8, 512]) for i in range(1)]
        TBb = [sb(f"a_TBb{i}", [128, 8, 512], BF16) for i in range(2)]
        ident_f = sb("a_identf", [128, 128])
        ident = sb("a_ident", [128, 128], BF16)
        P.dma("sp", "D_aid", lambda e: e.dma_start(out=ident_f[:], in_=K.eye_d), writes=["a_identf"])
        P.op("dve", lambda e: e.tensor_copy(out=ident[:], in_=ident_f[:]), reads=["a_identf"], writes=["a_ident"])
        pt = [sb(f"a_p{i}", [128, 512], BF16) for i in range(3)]
        rec = [sb(f"a_rec{i}", [128, 512]) for i in range(2)]
        P.dma("sp", "D_aMA", lambda e: e.dma_start(out=MA[:], in_=MA_d), writes=["a_MA"])
        for i in range(2):
            P.op("pool", lambda e, i=i: e.memset(vaug[i][:], 1.0), writes=[f"a_va{i}"])
            P.op("pool", lambda e, i=i: e.memset(q2[i][:], 0.0), writes=[f"a_q{i}"])
        ci = {"hk": 0, "t": 0, "p": 0, "rec": 0, "o": 0}
        pairs = sorted(set(h // 2 for h in heads))
        for jn, j in enumerate(pairs):
            qv, kv, vr, va = q2[jn % 2], k2[jn % 2], vraw[jn % 2], vaug[jn % 2]
            qk, kk, vrk, vak = f"a_q{jn % 2}", f"a_k{jn % 2}", f"a_vr{jn % 2}", f"a_va{jn % 2}"
            P.dma("sp", "D_" + qk, lambda e, qv=qv, j=j: e.dma_start(out=qv[0:64, 0, :], in_=QA[j * 128:j * 128 + 64, :]), writes=[qk])
            P.dma("sp", "D_" + qk, lambda e, qv=qv, j=j: e.dma_start(out=qv[64:128, 1, :], in_=QA[j * 128 + 64:(j + 1) * 128, :]), writes=[qk])
            P.dma("sp", "D_" + kk, lambda e, kv=kv, j=j: e.dma_start(out=kv[:], in_=KA[j * 128:(j + 1) * 128, :]), writes=[kk])
            vsrc = VA[:, j * 128:(j + 1) * 128].rearrange("(b p) f -> p b f", p=128)
            P.dma("sp", "D_" + vrk, lambda e, vr=vr, vsrc=vsrc: e.dma_start(out=vr[:], in_=vsrc), writes=[vrk])
            P.op("pool", lambda e, vr=vr, va=va: e.tensor_copy(out=va[:, :, 0, 0:64], in_=vr[:, :, 0:64]), reads=[vrk], writes=[vak])
            P.op("pool", lambda e, vr=vr, va=va: e.tensor_copy(out=va[:, :, 1, 64:128], in_=vr[:, :, 64:128]), reads=[vrk], writes=[vak])
            for e_ in range(2):
                h = 2 * j + e_
                if h not in heads:
                    continue
                tbv = TB[0]
                tbk = "a_TB0"
                tbb = TBb[h % 2]
                tbbk = f"a_TBb{h % 2}"
                for b in range(8):
                    ci["hk"] += 1
                    hv = hk[ci["hk"] % 2]
                    hkk = f"a_hk{ci['hk'] % 2}"
                    hsrc = bass.AP(RVR, h * 1536 + 128 * b, [[1, 128], [1, 512]])
                    P.dma("sp", "D_" + hkk, lambda e, hv=hv, hsrc=hsrc: e.dma_start(out=hv[:], in_=hsrc), writes=[hkk])
                    rev = bass.AP(hv, 511, [[hv[:].ap[0][0], 128], [-1, 512]])
                    P.op("dve", lambda e, rev=rev, b=b, tbv=tbv: e.tensor_tensor(out=tbv[:, b, :], in0=rev, in1=MA[:, b, :], op=ALU.add),
                         reads=[hkk, "a_MA"], writes=[tbk + f"_{b}"])
                    P.op("dve", lambda e, b=b, tbv=tbv, tbb=tbb: e.tensor_scalar(out=tbb[:, b, :], in0=tbv[:, b, :], scalar1=8.0, scalar2=None, op0=ALU.mult),
                         reads=[tbk + f"_{b}"], writes=[tbbk + f"_{b}"])
                ps_, pe_ = e_ * 64, (e_ + 1) * 64
                ss_, se_ = (1 - e_) * 64, (2 - e_) * 64
                LOOK = 2
                bo_of = {}
                for qt in qts:
                    ci["o"] += 1
                    bo_of[qt] = 4 + ci["o"] % 2
                pend = []

                def emit_SE(qt, b):
                    kblk = 4 * qt + b
                    bsc = nb(K, (0, 1, 2, 3))
                    P.op("pe", lambda e, bsc=bsc, kblk=kblk, qt=qt, kv=kv, qv=qv, e_=e_: e.matmul(
                        pb[bsc][:], kv[:, kblk * 128:(kblk + 1) * 128], qv[:, e_, qt * 512:(qt + 1) * 512], start=True, stop=False),
                        reads=[kk, qk], writes=[f"pb{bsc}"], inc=False)
                    P.op("pe", lambda e, bsc=bsc, b=b, tbb=tbb: e.matmul(pb[bsc][:], ident[:], tbb[:, b, :], start=False, stop=True),
                         reads=["a_ident", tbbk + f"_{b}"], writes=[f"pb{bsc}"])
                    ci["p"] += 1
                    pv = pt[ci["p"] % 3]
                    pk = f"a_p{ci['p'] % 3}"
                    bc = PC["oth"] if (qt == 0 and b < 4) else PC["zero"]
                    P.op("act", lambda e, pv=pv, bsc=bsc, bc=bc: e.activation(out=pv[:], in_=pb[bsc][:], func=AF.Exp, bias=prm[:, bc:bc + 1], scale=0.125),
                         reads=[f"pb{bsc}", "prm"], writes=[pk])
                    return (qt, b, kblk, pv, pk)

                def emit_V(item):
                    qt, b, kblk, pv, pk = item
                    bo = bo_of[qt]
                    P.op("pe", lambda e, bo=bo, kblk=kblk, pv=pv, b=b, va=va, e_=e_: e.matmul(
                        pb[bo][:], va[:, kblk, e_, :], pv[:], start=(b == 0), stop=(b == 7)),
                        reads=[vak, pk], writes=[f"pb{bo}"])
                    if b == 7:
                        ci["rec"] += 1
                        rv = rec[ci["rec"] % 2]
                        rk = f"a_rec{ci['rec'] % 2}"
                        P.op("dve", lambda e, rv=rv, bo=bo, ss_=ss_, se_=se_: e.reciprocal(out=rv[ss_:se_, :], in_=pb[bo][ss_:se_, :]),
                             reads=[f"pb{bo}"], writes=[rk])
                        P.op("dve", lambda e, rv=rv, bo=bo, qt=qt, ps_=ps_, pe_=pe_, ss_=ss_, se_=se_, j=j: e.tensor_tensor(
                            out=oA[ps_:pe_, j, qt * 512:(qt + 1) * 512], in0=pb[bo][ps_:pe_, :], in1=rv[ss_:se_, :], op=ALU.mult),
                            reads=[f"pb{bo}", rk], writes=[f"oA_{h}_{qt}"])

                for qt in qts:
                    for b in range(8):
                        pend.append(emit_SE(qt, b))
                        if len(pend) > LOOK:
                            emit_V(pend.pop(0))
                while pend:
                    emit_V(pend.pop(0))
        P.end_phase()


MLA_SCALE = 192.0 ** -0.5


def mla_phase(K, CQ, CKV, KPE, w_q_b, w_kv_b, rope_d, MB_d, OB, heads=range(16), qts=range(4)):
    nc, P = K.nc, K.P
    pb = K.pb
    prm = K.prm
    with contextlib.ExitStack() as st:
        sb = lambda n, s, d=F32: st.enter_context(nc.sbuf_tensor(n, s, d))
        cq = sb("m_cq", [128, 4, HALF], BF16)
        ckv = sb("m_ckv", [128, 4, SEQ], BF16)
        kpe = sb("m_kpe", [64, SEQ], BF16)
        ropeq = sb("m_rope", [64, 2, HALF])
        MB = sb("m_MB", [128, 4, 512])
        qn = [sb(f"m_qn{i}", [128, HALF], BF16) for i in range(1)]
        qr = [sb(f"m_qr{i}", [64, HALF], BF16) for i in range(1)]
        kn = [sb(f"m_kn{i}", [128, SEQ], BF16) for i in range(1)]
        vh = [sb(f"m_vh{i}", [128, 32, 128], BF16) for i in range(1)]
        rt = [sb(f"m_rt{i}", [64, 512]) for i in range(2)]
        tt_ = [sb(f"m_t{i}", [128, 512]) for i in range(1)]
        pt = [sb(f"m_p{i}", [128, 512], BF16) for i in range(4)]
        rec = [sb(f"m_rec{i}", [128, 512]) for i in range(1)]
        ost = [sb(f"m_o{i}", [128, 512], BF16) for i in range(2)]
        P.dma("sp", "D_mcq", lambda e: e.dma_start(out=cq[:], in_=CQ.rearrange("(c p) t -> p c t", p=128)), writes=["m_cq"])
        P.dma("sp", "D_mckv", lambda e: e.dma_start(out=ckv[:], in_=CKV.rearrange("(c p) t -> p c t", p=128)), writes=["m_ckv"])
        P.dma("sp", "D_mkpe", lambda e: e.dma_start(out=kpe[:], in_=KPE), writes=["m_kpe"])
        P.dma("sp", "D_mrope", lambda e: e.dma_start(out=ropeq[:], in_=rope_d[:, :, HALF:SEQ]), writes=["m_rope"])
        P.dma("sp", "D_mMB", lambda e: e.dma_start(out=MB[:], in_=MB_d), writes=["m_MB"])
        ci = {"t": 0, "p": 0, "rec": 0, "o": 0, "os": 0}
        for hn, h in enumerate(heads):
            i2 = 0
            qnv, qrv, knv, vhv = qn[i2], qr[i2], kn[i2], vh[i2]
            qnk, qrk, knk, vhk = f"m_qn{i2}", f"m_qr{i2}", f"m_kn{i2}", f"m_vh{i2}"
            i = K.ring_i % K.NRING
            K.ring_i += 1
            wq = K.ring[i][:, 0:4 * 256].rearrange("p (k f) -> p k f", k=4)
            wqk = f"wr{i}"
            for (o, s0, n) in [(0, h * 192, 192), (192, h * 192 + 160, 32), (224, h * 192 + 128, 32)]:
                srcw = w_q_b[:, s0:s0 + n].rearrange("(k p) f -> p k f", p=128)
                P.dma("pool", f"D_wr{i}", lambda e, o=o, n=n, srcw=srcw, wq=wq: e.dma_start(out=wq[:, :, o:o + n], in_=srcw), writes=[wqk])
            wkv, wkvk = load_w(K, w_kv_b, 0, 4, h * 256, 256)
            for tt in range(4):
                cs = slice(tt * 512, (tt + 1) * 512)
                b = nb(K, (0, 1, 2, 3))
                mm_group(K, pb[b][:], [(wq[:, kc, 0:128], cq[:, kc, cs]) for kc in range(4)], reads=[wqk, "m_cq"], wkey=f"pb{b}")
                evac_copy(K, qnv[:, cs], pb[b][:], f"pb{b}", qnk)
                bx, bs = nb(K, (0, 1, 2, 3)), nb(K, (0, 1, 2, 3))
                mm_group(K, pb[bx][0:64, :], [(wq[:, kc, 128:192], cq[:, kc, cs]) for kc in range(4)], reads=[wqk, "m_cq"], wkey=f"pb{bx}")
                mm_group(K, pb[bs][0:64, :], [(wq[:, kc, 192:256], cq[:, kc, cs]) for kc in range(4)], reads=[wqk, "m_cq"], wkey=f"pb{bs}")
                P.op("dve", lambda e, bx=bx, cs=cs: e.tensor_tensor(out=rt[0][:], in0=pb[bx][0:64, :], in1=ropeq[:, 0, cs], op=ALU.mult),
                     reads=[f"pb{bx}", "m_rope"], writes=["m_rt0"])
                P.op("dve", lambda e, bs=bs, cs=cs: e.tensor_tensor(out=rt[1][:], in0=pb[bs][0:64, :], in1=ropeq[:, 1, cs], op=ALU.mult),
                     reads=[f"pb{bs}", "m_rope"], writes=["m_rt1"])
                P.op("pool", lambda e, qrv=qrv, cs=cs: e.tensor_tensor(out=qrv[:, cs], in0=rt[0][:], in1=rt[1][:], op=ALU.add),
                     reads=["m_rt0", "m_rt1"], writes=[qrk])
            for tt in range(8):
                cs = slice(tt * 512, (tt + 1) * 512)
                b = nb(K, (0, 1, 2, 3))
                mm_group(K, pb[b][:], [(wkv[:, kc, 0:128], ckv[:, kc, cs]) for kc in range(4)], reads=[wkvk, "m_ckv"], wkey=f"pb{b}")
                evac_copy(K, knv[:, cs], pb[b][:], f"pb{b}", knk)
            for g in range(8):
                b = nb(K, (0, 1, 2, 3))
                for bl in range(4):
                    blk = 4 * g + bl
                    mm_group(K, pb[b][:, bl * 128:(bl + 1) * 128], [(ckv[:, kc, blk * 128:(blk + 1) * 128], wkv[:, kc, 128:256]) for kc in range(4)],
                             reads=[wkvk, "m_ckv"], wkey=f"pb{b}")
                evac_copy(K, vhv[:, 4 * g:4 * g + 4, :], pb[b][:].rearrange("p (a f) -> p a f", a=4), f"pb{b}", vhk)
            LOOK = 2
            bank_of = {}
            for qt in qts:
                ci["o"] += 1
                bank_of[qt] = (4 + ci["o"] % 2, 6 + ci["o"] % 2)
            pend = []

            def emit_SE(qt, bi_, kb, nblk):
                qs = slice(qt * 512, (qt + 1) * 512)
                ks = slice(kb * 128, (kb + 1) * 128)
                bsc = nb(K, (0, 1, 2, 3))
                P.op("pe", lambda e, bsc=bsc, ks=ks, qs=qs: e.matmul(pb[bsc][:], knv[:, ks], qnv[:, qs], start=True, stop=False),
                     reads=[knk, qnk], writes=[f"pb{bsc}"], inc=False)
                P.op("pe", lambda e, bsc=bsc, ks=ks, qs=qs: e.matmul(pb[bsc][:], kpe[0:64, ks], qrv[0:64, qs], start=False, stop=True),
                     reads=["m_kpe", qrk], writes=[f"pb{bsc}"])
                ci["p"] += 1
                pv = pt[ci["p"] % 4]
                pk = f"m_p{ci['p'] % 4}"
                if kb < 16:
                    bc = PC["oth"]
                    P.op("act", lambda e, pv=pv, bsc=bsc, bc=bc: e.activation(out=pv[:], in_=pb[bsc][:], func=AF.Exp, bias=prm[:, bc:bc + 1], scale=MLA_SCALE),
                         reads=[f"pb{bsc}", "prm"], writes=[pk])
                elif kb - 16 >= 4 * qt:
                    jd = kb - 16 - 4 * qt
                    ci["t"] += 1
                    tv = tt_[0]
                    tk = "m_t0"
                    P.op("dve", lambda e, tv=tv, bsc=bsc, jd=jd: e.scalar_tensor_tensor(out=tv[:], in0=pb[bsc][:], scalar=MLA_SCALE, in1=MB[:, jd, :], op0=ALU.mult, op1=ALU.add),
                         reads=[f"pb{bsc}", "m_MB"], writes=[tk])
                    bc = PC["zero"]
                    P.op("act", lambda e, pv=pv, tv=tv, bc=bc: e.activation(out=pv[:], in_=tv[:], func=AF.Exp, bias=prm[:, bc:bc + 1], scale=1.0),
                         reads=[tk, "prm"], writes=[pk])
                else:
                    bc = PC["zero"]
                    P.op("act", lambda e, pv=pv, bsc=bsc, bc=bc: e.activation(out=pv[:], in_=pb[bsc][:], func=AF.Exp, bias=prm[:, bc:bc + 1], scale=MLA_SCALE),
                         reads=[f"pb{bsc}", "prm"], writes=[pk])
                return (qt, bi_, kb, nblk, pv, pk)

            def emit_V(item):
                qt, bi_, kb, nblk, pv, pk = item
                bo, bsum = bank_of[qt]
                P.op("pe", lambda e, bo=bo, kb=kb, pv=pv, bi_=bi_, nblk=nblk: e.matmul(pb[bo][:], vhv[:, kb, :], pv[:], start=(bi_ == 0), stop=(bi_ == nblk - 1)),
                     reads=[vhk, pk], writes=[f"pb{bo}"])
                P.op("pe", lambda e, bsum=bsum, pv=pv, bi_=bi_, nblk=nblk: e.matmul(pb[bsum][:], K.ones_b[:], pv[:], start=(bi_ == 0), stop=(bi_ == nblk - 1)),
                     reads=["ones_b", pk], writes=[f"pb{bsum}"])
                if bi_ == nblk - 1:
                    ci["rec"] += 1
                    rv = rec[0]
                    rk = "m_rec0"
                    P.op("dve", lambda e, rv=rv, bsum=bsum: e.reciprocal(out=rv[:], in_=pb[bsum][:]), reads=[f"pb{bsum}"], writes=[rk])
                    ci["os"] += 1
                    ov = ost[ci["os"] % 2]
                    ok_ = f"m_o{ci['os'] % 2}"
                    P.op("dve", lambda e, ov=ov, bo=bo, rv=rv: e.tensor_tensor(out=ov[:], in0=pb[bo][:], in1=rv[:], op=ALU.mult),
                         reads=[f"pb{bo}", f"pb{bsum}", rk], writes=[ok_])
                    d = OB[h * 128:(h + 1) * 128, qt * 512:(qt + 1) * 512]
                    P.dma("sp", "D_" + ok_, lambda e, d=d, ov=ov: e.dma_start(out=d, in_=ov[:]), reads=[ok_])

            for qt in qts:
                blocks = list(range(16)) + [16 + kb for kb in range(4 * qt + 4)]
                for bi_, kb in enumerate(blocks):
                    pend.append(emit_SE(qt, bi_, kb, len(blocks)))
                    if len(pend) > LOOK:
                        emit_V(pend.pop(0))
            while pend:
                emit_V(pend.pop(0))
        P.end_phase()


def mixer_phase(K, oA, OB, G, H1, w_o_a, w_o_b, w_out, H2, H2B, tts=range(4)):
    nc, P = K.nc, K.P
    pb = K.pb
    with contextlib.ExitStack() as st:
        sb = lambda n, s, d=F32: st.enter_context(nc.sbuf_tensor(n, s, d))
        ob = sb("x_ob", [128, NDC, 512], BF16)
        m = sb("x_m", [128, NDC, 512], BF16)
        z = sb("x_z", [128, NDC, 512])
        gst = [sb(f"x_g{i}", [128, 2, 512]) for i in range(2)]
        hst = [sb(f"x_h{i}", [128, 512]) for i in range(2)]
        ta = [sb(f"x_ta{i}", [128, 512]) for i in range(2)]
        tb_ = [sb(f"x_tb{i}", [128, 512]) for i in range(2)]
        hbs = [sb(f"x_hb{i}", [128, 512], BF16) for i in range(2)]
        K.lnt = {"acc1": sb("x_acc1", [128, 512]), "acc2": sb("x_acc2", [128, 512]), "sq": sb("x_sq", [128, 512]),
                 "t": [sb(f"x_t{i}", [128, 512]) for i in range(2)]}
        ci = {"g": 0, "h": 0, "t": 0, "hb": 0}
        for tt in tts:
            cs = slice(tt * 512, (tt + 1) * 512)
            P.dma("sp", "D_xob", lambda e, cs=cs: e.dma_start(out=ob[:], in_=OB[:, cs].rearrange("(c p) t -> p c t", p=128)), writes=["x_ob"])
            for sl in range(4):
                woa, woak = load_w(K, w_o_a, 0, 8, sl * 512, 512)
                wob, wobk = load_w(K, w_o_b, 0, NDC, sl * 512, 512)
                for dl in range(4):
                    c = sl * 4 + dl
                    ci["g"] += 1
                    gv = gst[ci["g"] % 2]
                    gk = f"x_g{ci['g'] % 2}"
                    gsrc = G[:, cs].rearrange("(a c p) t -> c p a t", a=2, p=128)[c]
                    P.dma("sp", "D_" + gk, lambda e, gv=gv, gsrc=gsrc: e.dma_start(out=gv[:], in_=gsrc), writes=[gk])
                    ba, bb = nb(K, (0, 1, 2, 3)), nb(K, (0, 1, 2, 3))
                    mm_group(K, pb[ba][:], [(woa[:, kc, dl * 128:(dl + 1) * 128], oA[:, kc, cs]) for kc in range(8)], reads=[woak, "oA"], wkey=f"pb{ba}")
                    mm_group(K, pb[bb][:], [(wob[:, kc, dl * 128:(dl + 1) * 128], ob[:, kc, :]) for kc in range(NDC)], reads=[wobk, "x_ob"], wkey=f"pb{bb}")
                    ci["t"] += 1
                    tav, tbv = ta[ci["t"] % 2], tb_[ci["t"] % 2]
                    tak, tbk = f"x_ta{ci['t'] % 2}", f"x_tb{ci['t'] % 2}"
                    P.op("dve", lambda e, tav=tav, ba=ba, gv=gv: e.tensor_tensor(out=tav[:], in0=pb[ba][:], in1=gv[:, 0, :], op=ALU.mult),
                         reads=[f"pb{ba}", gk], writes=[tak])
                    P.op("dve", lambda e, tbv=tbv, bb=bb, gv=gv: e.tensor_tensor(out=tbv[:], in0=pb[bb][:], in1=gv[:, 1, :], op=ALU.mult),
                         reads=[f"pb{bb}", gk], writes=[tbk])
                    P.op("dve", lambda e, tav=tav, tbv=tbv, c=c: e.tensor_tensor(out=m[:, c, :], in0=tav[:], in1=tbv[:], op=ALU.add),
                         reads=[tak, tbk], writes=[f"x_m{c}"])
            for sl in range(4):
                wo, wok = load_w(K, w_out, 0, NDC, sl * 512, 512)
                for dl in range(4):
                    c = sl * 4 + dl
                    ci["h"] += 1
                    hv = hst[ci["h"] % 2]
                    hk_ = f"x_h{ci['h'] % 2}"
                    P.dma("sp", "D_" + hk_, lambda e, hv=hv, c=c, cs=cs: e.dma_start(out=hv[:], in_=H1[c * 128:(c + 1) * 128, cs]), writes=[hk_])
                    b = nb(K, (4, 5, 6, 7))
                    mm_group(K, pb[b][:], [(wo[:, kc, dl * 128:(dl + 1) * 128], m[:, kc, :]) for kc in range(NDC)],
                             reads=[wok] + [f"x_m{kc}" for kc in range(NDC)], wkey=f"pb{b}")
                    P.op("dve", lambda e, b=b, c=c, hv=hv: e.scalar_tensor_tensor(out=z[:, c, :], in0=pb[b][:], scalar=C_ONE, in1=hv[:], op0=ALU.mult, op1=ALU.add),
                         reads=[f"pb{b}", hk_], writes=[f"x_z{c}"])

            def after(c, cs=cs):
                ci["hb"] += 1
                hv2 = hbs[ci["hb"] % 2]
                hk2 = f"x_hb{ci['hb'] % 2}"
                P.op("dve", lambda e, hv2=hv2, c=c: e.tensor_copy(out=hv2[:], in_=z[:, c, :]), reads=[f"x_z{c}"], writes=[hk2])
                d = H2B[c * 128:(c + 1) * 128, cs]
                P.dma("sp", "D_" + hk2, lambda e, hv2=hv2, d=d: e.dma_start(out=d, in_=hv2[:]), reads=[hk2])
            ln_stats_and_norm(K, z, "x_z", 512, PC["ln2_g"], PC["ln2_b"], PC["eps_dn"], after)
            d = H2[:, cs].rearrange("(c p) t -> p c t", p=128)
            P.dma("sp", "D_xzo", lambda e, d=d: e.dma_start(out=d, in_=z[:]), reads=[f"x_z{c}" for c in range(NDC)])
        P.end_phase()


def memkv_phase(K, memT, w_xkv, KX, VX):
    nc, P = K.nc, K.P
    pb = K.pb
    with contextlib.ExitStack() as st:
        sb = lambda n, s, d=F32: st.enter_context(nc.sbuf_tensor(n, s, d))
        z = sb("k_z", [128, NDC, 256])
        mb = sb("k_mb", [128, NDC, 256], BF16)
        K.lnt = {"acc1": sb("k_acc1", [128, 256]), "acc2": sb("k_acc2", [128, 256]), "sq": sb("k_sq", [128, 256]),
                 "t": [sb(f"k_t{i}", [128, 256]) for i in range(2)]}
        P.dma("sp", "D_kz", lambda e: e.dma_start(out=z[:], in_=memT.rearrange("(c p) t -> p c t", p=128)), writes=[f"k_z{c}" for c in range(NDC)])

        def after(c):
            P.op("dve", lambda e, c=c: e.tensor_copy(out=mb[:, c, :], in_=z[:, c, :]), reads=[f"k_z{c}"], writes=["k_mb"])
        ln_stats_and_norm(K, z, "k_z", 256, PC["mln_g"], PC["mln_b"], PC["eps_ln"], after)
        for sl in range(4):
            wv, wk = load_w(K, w_xkv, 0, NDC, sl * 512, 512)
            for jj in range(4):
                b = nb(K, (0, 1, 2, 3))
                mm_group(K, pb[b][:, 0:256], [(wv[:, kc, jj * 128:(jj + 1) * 128], mb[:, kc, :]) for kc in range(NDC)], reads=[wk, "k_mb"], wkey=f"pb{b}")
                evac_copy(K, KX[:, sl * 4 + jj, :], pb[b][:, 0:256], f"pb{b}", "KX")
        for sl in range(4):
            wv, wk = load_w(K, w_xkv, 0, NDC, D + sl * 512, 512)
            for mbk in range(2):
                b = nb(K, (0, 1, 2, 3))
                mm_group(K, pb[b][:], [(mb[:, kc, mbk * 128:(mbk + 1) * 128], wv[:, kc, :]) for kc in range(NDC)], reads=[wk, "k_mb"], wkey=f"pb{b}")
                evac_copy(K, VX[:, mbk, sl * 512:(sl + 1) * 512], pb[b][:], f"pb{b}", "VX")
        P.end_phase()


X_SCALE = 512.0 ** -0.5


def cross_phase(K, H2, H2B, w_xq, w_xo, KX, VX, H3, H3B, tts=range(4)):
    nc, P = K.nc, K.P
    pb = K.pb
    prm = K.prm
    with contextlib.ExitStack() as st:
        sb = lambda n, s, d=F32: st.enter_context(nc.sbuf_tensor(n, s, d))
        hb = sb("c_hb", [128, NDC, 512], BF16)
        qx = sb("c_qx", [128, NDC, 512], BF16)
        ox = sb("c_ox", [128, NDC, 512], BF16)
        z = sb("c_z", [128, NDC, 512])
        hst = [sb(f"c_h{i}", [128, 512]) for i in range(2)]
        pt = [sb(f"c_p{i}", [128, 2, 512], BF16) for i in range(2)]
        rec = [sb(f"c_rec{i}", [128, 512]) for i in range(2)]
        hbs = [sb(f"c_hbs{i}", [128, 512], BF16) for i in range(2)]
        K.lnt = {"acc1": sb("c_acc1", [128, 512]), "acc2": sb("c_acc2", [128, 512]), "sq": sb("c_sq", [128, 512]),
                 "t": [sb(f"c_t{i}", [128, 512]) for i in range(2)]}
        ci = {"h": 0, "hb": 0}
        zc = PC["zero"]
        for tt in tts:
            cs = slice(tt * 512, (tt + 1) * 512)
            P.dma("sp", "D_chb", lambda e, cs=cs: e.dma_start(out=hb[:], in_=H2B[:, cs].rearrange("(c p) t -> p c t", p=128)), writes=["c_hb"])

            def evq(b, j, t_):
                evac_copy(K, qx[:, j, :], pb[b][:], f"pb{b}", f"c_qx{j}")
            fm_linear(K, w_xq, 0, NDC, hb, "c_hb", [0], evq)
            for hx in range(4):
                pv = pt[hx % 2]
                pk = f"c_p{hx % 2}"
                rv = rec[hx % 2]
                rk = f"c_rec{hx % 2}"
                for mbk in range(2):
                    b = nb(K, (0, 1, 2, 3))
                    mm_group(K, pb[b][:], [(KX[:, hx * 4 + cc, mbk * 128:(mbk + 1) * 128], qx[:, hx * 4 + cc, :]) for cc in range(4)],
                             reads=["KX"] + [f"c_qx{hx * 4 + cc}" for cc in range(4)], wkey=f"pb{b}")
                    P.op("act", lambda e, pv=pv, b=b, mbk=mbk: e.activation(out=pv[:, mbk, :], in_=pb[b][:], func=AF.Exp, bias=prm[:, zc:zc + 1], scale=X_SCALE),
                         reads=[f"pb{b}", "prm"], writes=[pk])
                bs_ = nb(K, (4, 5, 6, 7))
                mm_group(K, pb[bs_][:], [(K.ones_b[:], pv[:, mbk, :]) for mbk in range(2)], reads=["ones_b", pk], wkey=f"pb{bs_}")
                P.op("dve", lambda e, rv=rv, bs_=bs_: e.reciprocal(out=rv[:], in_=pb[bs_][:]), reads=[f"pb{bs_}"], writes=[rk])
                for cc in range(4):
                    b = nb(K, (4, 5, 6, 7))
                    col = hx * 512 + cc * 128
                    mm_group(K, pb[b][:], [(VX[:, mbk, col:col + 128], pv[:, mbk, :]) for mbk in range(2)], reads=["VX", pk], wkey=f"pb{b}")
                    P.op("dve", lambda e, b=b, hx=hx, cc=cc, rv=rv: e.tensor_tensor(out=ox[:, hx * 4 + cc, :], in0=pb[b][:], in1=rv[:], op=ALU.mult),
                         reads=[f"pb{b}", rk], writes=[f"c_ox{hx * 4 + cc}"])
            for sl in range(4):
                wo, wok = load_w(K, w_xo, 0, NDC, sl * 512, 512)
                for dl in range(4):
                    c = sl * 4 + dl
                    ci["h"] += 1
                    hv = hst[ci["h"] % 2]
                    hk_ = f"c_h{ci['h'] % 2}"
                    P.dma("sp", "D_" + hk_, lambda e, hv=hv, c=c, cs=cs: e.dma_start(out=hv[:], in_=H2[c * 128:(c + 1) * 128, cs]), writes=[hk_])
                    b = nb(K, (0, 1, 2, 3))
                    mm_group(K, pb[b][:], [(wo[:, kc, dl * 128:(dl + 1) * 128], ox[:, kc, :]) for kc in range(NDC)],
                             reads=[wok] + [f"c_ox{kc}" for kc in range(NDC)], wkey=f"pb{b}")
                    P.op("dve", lambda e, b=b, c=c, hv=hv: e.scalar_tensor_tensor(out=z[:, c, :], in0=pb[b][:], scalar=C_ONE, in1=hv[:], op0=ALU.mult, op1=ALU.add),
                         reads=[f"pb{b}", hk_], writes=[f"c_z{c}"])

            def after(c, cs=cs):
                ci["hb"] += 1
                hv2 = hbs[ci["hb"] % 2]
                hk2 = f"c_hbs{ci['hb'] % 2}"
                P.op("dve", lambda e, hv2=hv2, c=c: e.tensor_copy(out=hv2[:], in_=z[:, c, :]), reads=[f"c_z{c}"], writes=[hk2])
                d = H3B[c * 128:(c + 1) * 128, cs]
                P.dma("sp", "D_" + hk2, lambda e, hv2=hv2, d=d: e.dma_start(out=d, in_=hv2[:]), reads=[hk2])
            ln_stats_and_norm(K, z, "c_z", 512, PC["ln3_g"], PC["ln3_b"], PC["eps_dn"], after)
            d = H3[:, cs].rearrange("(c p) t -> p c t", p=128)
            P.dma("sp", "D_czo", lambda e, d=d: e.dma_start(out=d, in_=z[:]), reads=[f"c_z{c}" for c in range(NDC)])
        P.end_phase()


def mixer_m_phase(K, oA, OB, G, w_o_a, w_o_b, M):
    nc, P = K.nc, K.P
    pb = K.pb
    with contextlib.ExitStack() as st:
        sb = lambda n, s, d=F32: st.enter_context(nc.sbuf_tensor(n, s, d))
        ob = sb("y_ob", [128, NDC, 1024], BF16)
        gst = [sb(f"y_g{i}", [128, 2, 512]) for i in range(2)]
        ta = [sb(f"y_ta{i}", [128, 512]) for i in range(2)]
        tb_ = [sb(f"y_tb{i}", [128, 512]) for i in range(2)]
        mst = [sb(f"y_m{i}", [128, 512], BF16) for i in range(3)]
        ci = {"g": 0, "t": 0, "m": 0}
        for T in range(2):
            P.dma("sp", "D_yob", lambda e, T=T: e.dma_start(out=ob[:], in_=OB[:, T * 1024:(T + 1) * 1024].rearrange("(c p) t -> p c t", p=128)),
                  writes=["y_ob"])
            for sl in range(4):
                woa, woak = load_w(K, w_o_a, 0, 8, sl * 512, 512)
                wob, wobk = load_w(K, w_o_b, 0, NDC, sl * 512, 512)
                for dl in range(4):
                    c = sl * 4 + dl
                    for hf in range(2):
                        g0 = T * 1024 + hf * 512
                        cs = slice(g0, g0 + 512)
                        ls = slice(hf * 512, (hf + 1) * 512)
                        ci["g"] += 1
                        gv = gst[ci["g"] % 2]
                        gk = f"y_g{ci['g'] % 2}"
                        gsrc = G[:, cs].rearrange("(a c p) t -> c p a t", a=2, p=128)[c]
                        P.dma("sp", "D_" + gk, lambda e, gv=gv, gsrc=gsrc: e.dma_start(out=gv[:], in_=gsrc), writes=[gk])
                        ba, bb = nb(K, (0, 1, 2, 3)), nb(K, (0, 1, 2, 3))
                        mm_group(K, pb[ba][:], [(woa[:, kc, dl * 128:(dl + 1) * 128], oA[:, kc, cs]) for kc in range(8)], reads=[woak, "oA"], wkey=f"pb{ba}")
                        mm_group(K, pb[bb][:], [(wob[:, kc, dl * 128:(dl + 1) * 128], ob[:, kc, ls]) for kc in range(NDC)], reads=[wobk, "y_ob"], wkey=f"pb{bb}")
                        ci["t"] += 1
                        tav, tbv = ta[ci["t"] % 2], tb_[ci["t"] % 2]
                        tak, tbk = f"y_ta{ci['t'] % 2}", f"y_tb{ci['t'] % 2}"
                        P.op("dve", lambda e, tav=tav, ba=ba, gv=gv: e.tensor_tensor(out=tav[:], in0=pb[ba][:], in1=gv[:, 0, :], op=ALU.mult),
                             reads=[f"pb{ba}", gk], writes=[tak])
                        P.op("dve", lambda e, tbv=tbv, bb=bb, gv=gv: e.tensor_tensor(out=tbv[:], in0=pb[bb][:], in1=gv[:, 1, :], op=ALU.mult),
                             reads=[f"pb{bb}", gk], writes=[tbk])
                        ci["m"] += 1
                        mv = mst[ci["m"] % 3]
                        mk = f"y_m{ci['m'] % 3}"
                        P.op("dve", lambda e, tav=tav, tbv=tbv, mv=mv: e.tensor_tensor(out=mv[:], in0=tav[:], in1=tbv[:], op=ALU.add),
                             reads=[tak, tbk], writes=[mk])
                        d = M[c * 128:(c + 1) * 128, cs]
                        P.dma("sp", "D_" + mk, lambda e, mv=mv, d=d: e.dma_start(out=d, in_=mv[:]), reads=[mk])
        P.end_phase()


def out_ln_phase(K, ACT, W, RES, gcol, bcol, OUTF, OUTB):
    nc, P = K.nc, K.P
    pb = K.pb
    K.uid = getattr(K, "uid", 0) + 1
    with contextlib.ExitStack() as st:
        sb = lambda n, s, d=F32: st.enter_context(nc.sbuf_tensor(f"u{K.uid}_" + n, s, d))
        act = sb("o_act", [128, NDC, 1024], BF16)
        z = sb("o_z", [128, NDC, 1024])
        hst = [sb(f"o_h{i}", [128, 512]) for i in range(2)]
        hbs = [sb(f"o_hb{i}", [128, 1024], BF16) for i in range(2)]
        K.lnt = {"acc1": sb("o_acc1", [128, 1024]), "acc2": sb("o_acc2", [128, 1024]), "sq": sb("o_sq", [128, 1024]),
                 "t": [sb(f"o_t{i}", [128, 1024]) for i in range(2)]}
        ci = {"h": 0, "hb": 0}
        for T in range(2):
            P.dma("sp", "D_oact", lambda e, T=T: e.dma_start(out=act[:], in_=ACT[:, T * 1024:(T + 1) * 1024].rearrange("(c p) t -> p c t", p=128)),
                  writes=["o_act"])
            for sl in range(4):
                wo, wok = load_w(K, W, 0, NDC, sl * 512, 512)
                for dl in range(4):
                    c = sl * 4 + dl
                    for hf in range(2):
                        g0 = T * 1024 + hf * 512
                        ls = slice(hf * 512, (hf + 1) * 512)
                        ci["h"] += 1
                        hv = hst[ci["h"] % 2]
                        hk_ = f"o_h{ci['h'] % 2}"
                        P.dma("sp", "D_" + hk_, lambda e, hv=hv, c=c, g0=g0: e.dma_start(out=hv[:], in_=RES[c * 128:(c + 1) * 128, g0:g0 + 512]), writes=[hk_])
                        b = nb(K, (4, 5, 6, 7))
                        mm_group(K, pb[b][:], [(wo[:, kc, dl * 128:(dl + 1) * 128], act[:, kc, ls]) for kc in range(NDC)], reads=[wok, "o_act"], wkey=f"pb{b}")
                        P.op("dve", lambda e, b=b, c=c, hv=hv, ls=ls: e.scalar_tensor_tensor(out=z[:, c, ls], in0=pb[b][:], scalar=C_ONE, in1=hv[:], op0=ALU.mult, op1=ALU.add),
                             reads=[f"pb{b}", hk_], writes=[f"o_z{c}"])

            def after(c, T=T):
                ci["hb"] += 1
                hv2 = hbs[ci["hb"] % 2]
                hk2 = f"o_hb{ci['hb'] % 2}"
                P.op("dve", lambda e, hv2=hv2, c=c: e.tensor_copy(out=hv2[:], in_=z[:, c, :]), reads=[f"o_z{c}"], writes=[hk2])
                d = OUTB[c * 128:(c + 1) * 128, T * 1024:(T + 1) * 1024]
                P.dma("sp", "D_" + hk2, lambda e, hv2=hv2, d=d: e.dma_start(out=d, in_=hv2[:]), reads=[hk2])
            ln_stats_and_norm(K, z, "o_z", 1024, gcol, bcol, PC["eps_dn"], after)
            d = OUTF[:, T * 1024:(T + 1) * 1024].rearrange("(c p) t -> p c t", p=128)
            P.dma("sp", "D_ozo", lambda e, d=d: e.dma_start(out=d, in_=z[:]), reads=[f"o_z{c}" for c in range(NDC)])
        P.end_phase()


def cross_q_phase(K, H2B, w_xq, KX, VX, OX):
    nc, P = K.nc, K.P
    pb = K.pb
    prm = K.prm
    with contextlib.ExitStack() as st:
        sb = lambda n, s, d=F32: st.enter_context(nc.sbuf_tensor(n, s, d))
        hb = sb("d_hb", [128, NDC, 1024], BF16)
        qx = sb("d_qx", [128, NDC, 1024], BF16)
        pt = [sb(f"d_p{i}", [128, 2, 512], BF16) for i in range(2)]
        rec = [sb(f"d_rec{i}", [128, 512]) for i in range(2)]
        ost = [sb(f"d_o{i}", [128, 512], BF16) for i in range(3)]
        ci = {"o": 0, "x": 0}
        zc = PC["zero"]
        for T in range(2):
            P.dma("sp", "D_dhb", lambda e, T=T: e.dma_start(out=hb[:], in_=H2B[:, T * 1024:(T + 1) * 1024].rearrange("(c p) t -> p c t", p=128)),
                  writes=["d_hb"])

            def evq(b, j, t_):
                evac_copy(K, qx[:, j, t_ * 512:(t_ + 1) * 512], pb[b][:], f"pb{b}", f"d_qx{j}_{t_}")
            fm_linear(K, w_xq, 0, NDC, hb, "d_hb", [0, 1], evq)
            for hf in range(2):
                ls = slice(hf * 512, (hf + 1) * 512)
                g0 = T * 1024 + hf * 512
                for hx in range(4):
                    ci["x"] += 1
                    pv = pt[ci["x"] % 2]
                    pk = f"d_p{ci['x'] % 2}"
                    rv = rec[ci["x"] % 2]
                    rk = f"d_rec{ci['x'] % 2}"
                    for mbk in range(2):
                        b = nb(K, (0, 1, 2, 3))
                        mm_group(K, pb[b][:], [(KX[:, hx * 4 + cc, mbk * 128:(mbk + 1) * 128], qx[:, hx * 4 + cc, ls]) for cc in range(4)],
                                 reads=["KX"] + [f"d_qx{hx * 4 + cc}_{hf}" for cc in range(4)], wkey=f"pb{b}")
                        P.op("act", lambda e, pv=pv, b=b, mbk=mbk: e.activation(out=pv[:, mbk, :], in_=pb[b][:], func=AF.Exp, bias=prm[:, zc:zc + 1], scale=X_SCALE),
                             reads=[f"pb{b}", "prm"], writes=[pk])
                    bs_ = nb(K, (4, 5, 6, 7))
                    mm_group(K, pb[bs_][:], [(K.ones_b[:], pv[:, mbk, :]) for mbk in range(2)], reads=["ones_b", pk], wkey=f"pb{bs_}")
                    P.op("dve", lambda e, rv=rv, bs_=bs_: e.reciprocal(out=rv[:], in_=pb[bs_][:]), reads=[f"pb{bs_}"], writes=[rk])
                    for cc in range(4):
                        b = nb(K, (4, 5, 6, 7))
                        col = hx * 512 + cc * 128
                        mm_group(K, pb[b][:], [(VX[:, mbk, col:col + 128], pv[:, mbk, :]) for mbk in range(2)], reads=["VX", pk], wkey=f"pb{b}")
                        ci["o"] += 1
                        ov = ost[ci["o"] % 3]
                        ok_ = f"d_o{ci['o'] % 3}"
                        P.op("dve", lambda e, b=b, ov=ov, rv=rv: e.tensor_tensor(out=ov[:], in0=pb[b][:], in1=rv[:], op=ALU.mult),
                             reads=[f"pb{b}", rk], writes=[ok_])
                        d = OX[(hx * 4 + cc) * 128:(hx * 4 + cc + 1) * 128, g0:g0 + 512]
                        P.dma("sp", "D_" + ok_, lambda e, ov=ov, d=d: e.dma_start(out=d, in_=ov[:]), reads=[ok_])
        P.end_phase()


def build(mode="full", ext=None):
    ext = ext or {}
    ph = set(mode.split("+"))
    full = "full" in ph
    on = lambda p: full or p in ph
    nc = bass.Bass("TRN2", target_bir_lowering=False)
    K = Ctx()
    K.nc = nc
    K.bi = 0
    K.ev = 0
    TH = {}

    def dram(name, shape, dt, kind=None):
        kind = kind or ext.get(name, "Internal")
        t = nc.dram_tensor(name, shape, dt, kind=kind)
        TH[name] = t
        return t.ap()

    IN = "ExternalInput"
    xT = dram("xT", [D, SEQ], F32, IN)
    memT = dram("memT", [D, 256], F32, IN)
    prm_d = dram("prm", [128, NPRM], F32, IN)
    rope_d = dram("rope", [64, 2, SEQ], F32, IN)
    dram("rvr", [16, 1536], F32, IN)
    MA_d = dram("MA", [128, 8, 512], F32, IN)
    MB_d = dram("MB", [128, 4, 512], F32, IN)
    K.eye_d = dram("eye", [128, 128], F32, IN)
    w = {}
    for n, shp in [("ffn1_w_in", [D, 2 * DFF]), ("ffn1_w_out", [DFF, D]), ("w_in", [D, 8256]), ("w_q_b", [512, 3072]),
                   ("w_kv_b", [512, 4096]), ("w_o_a", [1024, D]), ("w_o_b", [D, D]), ("w_out", [D, D]), ("w_xq", [D, D]),
                   ("w_xkv", [D, 2 * D]), ("w_xo", [D, D]), ("ffn2_w_in", [D, 2 * DFF]), ("ffn2_w_out", [DFF, D])]:
        w[n] = dram(n, shp, F32, IN)
    H1 = dram("H1", [D, HALF], F32)
    H1B = dram("H1B", [D, SEQ], BF16)
    QA = dram("QA", [1024, HALF], BF16)
    KA = dram("KA", [1024, 2560], BF16)
    VA = dram("VA", [2560, 1024], BF16)
    CQ = dram("CQ", [512, HALF], BF16)
    CKV = dram("CKV", [512, SEQ], BF16)
    KPE = dram("KPE", [64, SEQ], BF16)
    G = dram("G", [2 * D, HALF], F32)
    OB = dram("OB", [D, HALF], BF16)
    H2 = dram("H2", [D, HALF], F32)
    H2B = dram("H2B", [D, HALF], BF16)
    H3 = dram("H3", [D, HALF], F32)
    H3B = dram("H3B", [D, HALF], BF16)
    OUT = dram("out", [D, HALF], F32, "ExternalOutput")
    OAD = dram("OAD", [1024, HALF], BF16)
    MD = dram("MD", [D, HALF], BF16)
    OXD = dram("OXD", [D, HALF], BF16)

    with contextlib.ExitStack() as st:
        P = Prog(nc, st)
        K.P = P
        sb = lambda n, s, d=F32: st.enter_context(nc.sbuf_tensor(n, s, d))
        K.NRING = 3
        K.ring = [sb(f"wr{i}", [128, 8192], BF16) for i in range(K.NRING)]
        K.ring_i = 0
        K.pb = [st.enter_context(nc.psum_tensor(f"pb{i}", [128, 512], F32)) for i in range(8)]
        K.prm = sb("prm_sb", [128, NPRM])
        K.ones_f = sb("ones_f", [128, 128])
        K.ones_b = sb("ones_b", [128, 128], BF16)
        P.dma("sp", "D_prm", lambda e: e.dma_start(out=K.prm[:], in_=prm_d), writes=["prm"])
        P.op("dve", lambda e: e.memset(K.ones_f[:], 1.0), writes=["ones_f"])
        P.op("dve", lambda e: e.memset(K.ones_b[:], 1.0), writes=["ones_b"])
        P.end_phase()

        if on("ffn1"):
            tiles = ext.get("tiles1", [(0, None, 0), (1024, None, 1024), (2048, 0, 2048), (3072, 1024, 3072)])
            ffn_phase(K, tiles, xT, xT, True, w["ffn1_w_in"], w["ffn1_w_out"], PC["ln1_g"], PC["ln1_b"], H1, H1B)
        if on("proj"):
            proj_phase(K, H1B, w["w_in"], QA, KA, VA, CQ, CKV, KPE, G, rope_d, subs=ext.get("subs", ("own", "ctx")),
                       ntt_override=ext.get("proj_tts"))
        if on("attna") or on("mla") or on("mixer"):
            with contextlib.ExitStack() as st2:
                oA = st2.enter_context(nc.sbuf_tensor("oA", [128, 8, HALF], BF16))
                if on("attna"):
                    attnA_phase(K, QA, KA, VA, TH["rvr"], MA_d, oA, heads=ext.get("a_heads", range(16)), qts=ext.get("a_qts", range(4)))
                    if "OAD" in ext:
                        P.dma("sp", "D_oad", lambda e: e.dma_start(out=OAD.rearrange("(c p) t -> p c t", p=128), in_=oA[:]), final=True)
                        P.end_phase()
                if on("mla"):
                    mla_phase(K, CQ, CKV, KPE, w["w_q_b"], w["w_kv_b"], rope_d, MB_d, OB, heads=ext.get("m_heads", range(16)),
                              qts=ext.get("m_qts", range(4)))
                if on("mixer"):
                    if ext.get("OAD") == "ExternalInput":
                        P.dma("sp", "D_oad", lambda e: e.dma_start(out=oA[:], in_=OAD.rearrange("(c p) t -> p c t", p=128)), writes=["oA"])
                    mixer_m_phase(K, oA, OB, G, w["w_o_a"], w["w_o_b"], MD)
        if on("mixer"):
            out_ln_phase(K, MD, w["w_out"], H1, PC["ln2_g"], PC["ln2_b"], H2, H2B)
        if on("cross"):
            with contextlib.ExitStack() as st2:
                KX = st2.enter_context(nc.sbuf_tensor("KX", [128, NDC, 256], BF16))
                VX = st2.enter_context(nc.sbuf_tensor("VX", [128, 2, D], BF16))
                memkv_phase(K, memT, w["w_xkv"], KX, VX)
                cross_q_phase(K, H2B, w["w_xq"], KX, VX, OXD)
                out_ln_phase(K, OXD, w["w_xo"], H2, PC["ln3_g"], PC["ln3_b"], H3, H3B)
        if on("ffn2"):
            tiles = ext.get("tiles2", [(0, 0, None), (1024, 1024, None)])
            ffn_phase(K, tiles, H3, H3B, False, w["ffn2_w_in"], w["ffn2_w_out"], PC["ln4_g"], PC["ln4_b"], OUT, None)
        fw = {}
        for s, v in P.final_waits:
            fw[s] = max(fw.get(s, 0), v)
        P.ops["sp"].append((list(fw.items()), None, None))
        P.end_phase()
    return nc


def col_layout(v):
    v = np.asarray(v, np.float32).reshape(-1)
    return np.ascontiguousarray(v.reshape(-1, 128).T)


def make_prm(inputs, hi):
    prm = np.zeros((128, NPRM), np.float32)

    def put(name, v):
        c = col_layout(v)
        prm[:, PC[name]:PC[name] + c.shape[1]] = c
    put("ln1_g", inputs["ln_ffn1_g"][0]); put("ln1_b", inputs["ln_ffn1_b"][0])
    put("ln2_g", inputs["ln_mix_g"][0]); put("ln2_b", inputs["ln_mix_b"][0])
    put("ln3_g", inputs["ln_x_g"][0]); put("ln3_b", inputs["ln_x_b"][0])
    put("ln4_g", inputs["ln_ffn2_g"][0]); put("ln4_b", inputs["ln_ffn2_b"][0])
    put("mln_g", inputs["mem_ln_g"][0]); put("mln_b", inputs["mem_ln_b"][0])
    put("gate_b", inputs["gate_bias"][0])
    put("qan", inputs["q_a_norm"][0]); put("kvan", inputs["kv_a_norm"][0])
    prm[:, PC["oth"]] = 0.0 if hi else NEG
    prm[:, PC["eps_dn"]] = 1e-5 / (ALPHA * ALPHA)
    prm[:, PC["eps_ln"]] = 1e-5
    prm[:, PC["eps_rms"]] = 1e-6
    return prm


def make_rope(hi):
    inv = (1.0 / (np.float32(10000.0) ** (np.arange(0, 64, 2, dtype=np.float32) / np.float32(64)))).astype(np.float32)
    own = np.arange(HALF, SEQ) if hi else np.arange(0, HALF)
    pos = np.concatenate([np.arange(0, HALF), own]).astype(np.float32)
    ang = (pos[None, :] * inv[:, None]).astype(np.float32)
    cos, sin = np.cos(ang).astype(np.float32), np.sin(ang).astype(np.float32)
    tab = np.empty((64, 2, SEQ), np.float32)
    tab[0:32, 0] = cos
    tab[32:64, 0] = cos
    tab[0:32, 1] = -sin
    tab[32:64, 1] = sin
    return tab


def make_masks():
    k = np.arange(128)[:, None]
    q = np.arange(512)[None, :]
    MA = np.zeros((128, 8, 512), np.float32)
    for b in range(8):
        diff = 8 + q // 64 - 2 * b - k // 64
        MA[:, b, :] = np.where((diff >= 0) & (diff <= 8), 0.0, NEG)
    MB = np.zeros((128, 4, 512), np.float32)
    for j in range(4):
        MB[:, j, :] = np.where((128 * j + k) // 64 <= q // 64, 0.0, NEG)
    return MA, MB


def make_rvr(rel_bias):
    idx = np.clip(1023 - np.arange(1536), -128, 128) + 128
    return np.ascontiguousarray(np.asarray(rel_bias, np.float32)[:, idx])


WNAMES = ["ffn1_w_in", "ffn1_w_out", "w_in", "w_q_b", "w_kv_b", "w_o_a", "w_o_b", "w_out", "w_xq", "w_xkv", "w_xo",
          "ffn2_w_in", "ffn2_w_out"]


def core_inputs(inputs, c, shared=None):
    b, hf = divmod(c, 2)
    hi = hf == 1
    x = np.asarray(inputs["x"][b], np.float32)
    xT = np.zeros((D, SEQ), np.float32)
    if hi:
        xT[:] = x.T
    else:
        xT[:, HALF:] = x[0:HALF].T
    m = {"xT": xT, "memT": np.ascontiguousarray(np.asarray(inputs["mem"][b], np.float32).T),
         "prm": make_prm(inputs, hi), "rope": make_rope(hi)}
    if shared is None:
        shared = shared_inputs(inputs)
    m.update(shared)
    return m


def shared_inputs(inputs):
    MA, MB = make_masks()
    s = {"MA": MA, "MB": MB, "rvr": make_rvr(inputs["rel_bias"][0]), "eye": np.eye(128, dtype=np.float32)}
    for n in WNAMES:
        s[n] = np.ascontiguousarray(np.asarray(inputs[n][0], np.float32))
    return s


def kernel(**inputs):
    nc = build("full")
    shared = shared_inputs(inputs)
    in_maps = [core_inputs(inputs, c, shared) for c in range(8)]
    res = run_bass_kernel_spmd(nc, in_maps, core_ids=list(range(8)))
    out = np.empty((4, SEQ, D), np.float32)
    for c in range(8):
        b, hf = divmod(c, 2)
        out[b, hf * HALF:(hf + 1) * HALF, :] = np.asarray(res.results[c]["out"], np.float32).T
    return out
```

```python
import contextlib
import numpy as np
import ml_dtypes
import concourse.bass as bass
import concourse.mybir as mybir
from concourse.bass_utils import run_bass_kernel_spmd

F32 = mybir.dt.float32
BF16 = mybir.dt.bfloat16
AF = mybir.ActivationFunctionType
ALU = mybir.AluOpType

D = 2048
NDC = 16
SEQ = 4096
HALF = 2048
DFF = 5504
NFC = 43
ALPHA = 2.0 ** 0.25
C_FFN = 0.5 / ALPHA
C_ONE = 1.0 / ALPHA
NEG = -30000.0

PC = {}
_o = 0
for _n, _w in [("ln1_g", 16), ("ln1_b", 16), ("ln2_g", 16), ("ln2_b", 16), ("ln3_g", 16), ("ln3_b", 16),
               ("ln4_g", 16), ("ln4_b", 16), ("mln_g", 16), ("mln_b", 16), ("gate_b", 32), ("qan", 4),
               ("kvan", 4), ("oth", 1), ("eps_dn", 1), ("eps_ln", 1), ("eps_rms", 1), ("zero", 1)]:
    PC[_n] = _o
    _o += _w
NPRM = _o

ENGS = ("pe", "act", "dve", "pool", "sp")


class Prog:
    def __init__(self, nc, stack):
        self.nc = nc
        self.stack = stack
        self.ops = {e: [] for e in ENGS}
        self.count = {e: 0 for e in ENGS}
        self.semname = {e: "S_" + e for e in ENGS}
        self.handles = {}
        self.dma_sems = {}
        self.seen = {e: {} for e in ENGS}
        self.last_write = {}
        self.readers = {}
        self.final_waits = []
        for e in ENGS:
            self._sem(self.semname[e])

    def _sem(self, name):
        if name not in self.handles:
            self.handles[name] = self.stack.enter_context(self.nc.semaphore(name))
        return self.handles[name]

    def _need(self, eng, dep, waits):
        sem, val, deng = dep
        if deng == eng:
            return
        if self.seen[eng].get(sem, 0) >= val:
            return
        waits[sem] = max(waits.get(sem, 0), val)

    def _deps(self, eng, reads, writes):
        waits = {}
        for b in reads:
            lw = self.last_write.get(b)
            if lw is not None:
                self._need(eng, lw, waits)
        for b in writes:
            lw = self.last_write.get(b)
            if lw is not None:
                self._need(eng, lw, waits)
            for r in self.readers.get(b, ()):
                self._need(eng, r, waits)
        for sem, val in waits.items():
            self.seen[eng][sem] = val
        return list(waits.items())

    def _commit(self, token, reads, writes):
        for b in reads:
            self.readers.setdefault(b, []).append(token)
        for b in writes:
            self.last_write[b] = token
            self.readers[b] = []

    def op(self, eng, fn, reads=(), writes=(), inc=True):
        waits = self._deps(eng, reads, writes)
        if inc:
            self.count[eng] += 1
            token = (self.semname[eng], self.count[eng], eng)
            self.ops[eng].append((waits, fn, (self.semname[eng], 1)))
        else:
            token = (self.semname[eng], self.count[eng] + 1, eng)
            self.ops[eng].append((waits, fn, None))
        self._commit(token, reads, writes)
        return token

    def dma(self, queue, semname, fn, reads=(), writes=(), final=False):
        self._sem(semname)
        waits = self._deps(queue, reads, writes)
        prev = self.dma_sems.get(semname, 0)
        if prev and self.seen[queue].get(semname, 0) < prev:
            waits.append((semname, prev))
            self.seen[queue][semname] = prev
        self.dma_sems[semname] = prev + 16
        val = prev + 16
        token = (semname, val, "dma")
        self.ops[queue].append((waits, fn, (semname, 16)))
        self._commit(token, reads, writes)
        if final:
            self.final_waits.append((semname, val))
        return token

    def end_phase(self, last=False):
        toks = [(self.semname[e], self.count[e]) for e in ENGS if self.count[e]]
        toks += [(s, v) for s, v in self.dma_sems.items()]
        for e in ENGS:
            waits = []
            for s, v in toks:
                if s == self.semname[e]:
                    continue
                if self.seen[e].get(s, 0) < v:
                    waits.append((s, v))
                    self.seen[e][s] = v
            if waits:
                self.ops[e].append((waits, None, None))
        H = self.handles
        ops = self.ops

        def runner(e):
            def run(engine):
                for waits, fn, inc in ops[e]:
                    for s, v in waits:
                        engine.wait_ge(H[s], v)
                    if fn is None:
                        continue
                    ins = fn(engine)
                    if inc is not None:
                        ins.then_inc(H[inc[0]], inc[1])
            return run

        with self.nc.Block() as block:
            block.tensor(runner("pe"))
            block.scalar(runner("act"))
            block.vector(runner("dve"))
            block.gpsimd(runner("pool"))
            block.sync(runner("sp"))
        self.ops = {e: [] for e in ENGS}
        self.last_write = {}
        self.readers = {}


class Ctx:
    pass


def _cols(ap2d, c0, n):
    return ap2d[:, c0:c0 + n]


def load_w(K, W, r0, nk, c0, ncols, extra=None):
    P = K.P
    i = K.ring_i % K.NRING
    K.ring_i += 1
    slot = K.ring[i]
    key = f"wr{i}"
    view = slot[:, 0:nk * ncols].rearrange("p (k f) -> p k f", k=nk)
    src = W[r0:r0 + nk * 128, c0:c0 + ncols].rearrange("(k p) f -> p k f", p=128)
    P.dma("pool", f"D_wr{i}", lambda e: e.dma_start(out=view, in_=src), writes=[key])
    return view, key


def mm_group(K, out_ap, pairs, reads, wkey):
    n = len(pairs)
    for i, (l, r) in enumerate(pairs):
        K.P.op("pe", (lambda e, l=l, r=r, i=i: e.matmul(out_ap, l, r, start=(i == 0), stop=(i == n - 1))),
               reads=reads, writes=[wkey], inc=(i == n - 1))


def ln_stats_and_norm(K, z, zkey, nt, gcol, bcol, epscol, after_chunk, nchunks=NDC, dtot=D, rms=False,
                      banks=(0, 1, 2, 3), defer=False, parts=False):
    P = K.P
    T = K.lnt
    acc1, acc2, sq = T["acc1"], T["acc2"], T["sq"]
    prm = K.prm
    zk = [zkey + str(c) for c in range(nchunks)]
    nh = (nt + 511) // 512
    hw = min(nt, 512)
    def part_a(c):
        sqb = T["t"][c % 2]
        sqk = f"lnt{c % 2}"
        if c == 0:
            P.op("act", lambda e: e.activation(out=acc2[:, 0:nt], in_=z[:, 0, :], func=AF.Square), reads=[zk[0]], writes=["acc2"])
        else:
            P.op("act", lambda e, c=c, sqb=sqb: e.activation(out=sqb[:, 0:nt], in_=z[:, c, :], func=AF.Square), reads=[zk[c]], writes=[sqk])
            P.op("dve", lambda e, sqb=sqb: e.tensor_tensor(out=acc2[:, 0:nt], in0=acc2[:, 0:nt], in1=sqb[:, 0:nt], op=ALU.add),
                 reads=[sqk, "acc2"], writes=["acc2"])
        if not rms:
            if c == 1:
                P.op("dve", lambda e: e.tensor_tensor(out=acc1[:, 0:nt], in0=z[:, 0, :], in1=z[:, 1, :], op=ALU.add),
                     reads=zk[0:2], writes=["acc1"])
            elif c >= 2:
                P.op("dve", lambda e, c=c: e.tensor_tensor(out=acc1[:, 0:nt], in0=acc1[:, 0:nt], in1=z[:, c, :], op=ALU.add),
                     reads=[zk[c], "acc1"], writes=["acc1"])


    if parts:
        return (part_a,
                lambda: _ln_part_b(K, z, zk, nt, gcol, bcol, epscol, after_chunk, nchunks, dtot, rms, banks, nh, hw, True, ()),
                lambda c: _ln_part_b(K, z, zk, nt, gcol, bcol, epscol, after_chunk, nchunks, dtot, rms, banks, nh, hw, False, (c,)))
    for c in range(nchunks):
        part_a(c)

    def part_b():
        _ln_part_b(K, z, zk, nt, gcol, bcol, epscol, after_chunk, nchunks, dtot, rms, banks, nh, hw)
    if defer:
        return part_b
    part_b()


def _ln_part_b(K, z, zk, nt, gcol, bcol, epscol, after_chunk, nchunks, dtot, rms, banks, nh, hw, do_stats=True, chunks=None):
    P = K.P
    T = K.lnt
    acc1, acc2, sq = T["acc1"], T["acc2"], T["sq"]
    prm = K.prm
    pb = K.pb
    for h in (range(nh) if do_stats else ()):
        sl = slice(h * hw, (h + 1) * hw)
        if not rms:
            b1 = banks[(2 * h) % len(banks)]
            P.op("pe", lambda e, sl=sl, b1=b1: e.matmul(pb[b1][:, 0:hw], K.ones_f[:], acc1[:, sl], start=True, stop=True),
                 reads=["acc1", "ones_f"], writes=[f"pb{b1}"])
        b2 = banks[(2 * h + 1) % len(banks)]
        P.op("pe", lambda e, sl=sl, b2=b2: e.matmul(pb[b2][:, 0:hw], K.ones_f[:], acc2[:, sl], start=True, stop=True),
             reads=["acc2", "ones_f"], writes=[f"pb{b2}"])
        if not rms:
            P.op("dve", lambda e, sl=sl, b1=b1: e.tensor_scalar(out=acc1[:, sl], in0=pb[b1][:, 0:hw], scalar1=1.0 / dtot, scalar2=None, op0=ALU.mult),
                 reads=[f"pb{b1}"], writes=["acc1"])
            P.op("dve", lambda e, sl=sl: e.tensor_tensor(out=sq[:, sl], in0=acc1[:, sl], in1=acc1[:, sl], op=ALU.mult),
                 reads=["acc1"], writes=["sq"])
            P.op("dve", lambda e, sl=sl, b2=b2: e.scalar_tensor_tensor(out=acc2[:, sl], in0=pb[b2][:, 0:hw], scalar=1.0 / dtot, in1=sq[:, sl], op0=ALU.mult, op1=ALU.subtract),
                 reads=[f"pb{b2}", "sq"], writes=["acc2"])
            P.op("act", lambda e, sl=sl: e.activation(out=acc2[:, sl], in_=acc2[:, sl], func=AF.Sqrt, bias=prm[:, epscol:epscol + 1], scale=1.0),
                 reads=["acc2", "prm"], writes=["acc2"])
        else:
            P.op("act", lambda e, sl=sl, b2=b2: e.activation(out=acc2[:, sl], in_=pb[b2][:, 0:hw], func=AF.Sqrt, bias=prm[:, epscol:epscol + 1], scale=1.0 / dtot),
                 reads=[f"pb{b2}", "prm"], writes=["acc2"])
        P.op("dve", lambda e, sl=sl: e.reciprocal(out=acc2[:, sl], in_=acc2[:, sl]), reads=["acc2"], writes=["acc2"])
        if not rms:
            P.op("dve", lambda e, sl=sl: e.scalar_tensor_tensor(out=sq[:, sl], in0=acc1[:, sl], scalar=-1.0, in1=acc2[:, sl], op0=ALU.mult, op1=ALU.mult),
                 reads=["acc1", "acc2"], writes=["sq"])
    for c in (range(nchunks) if chunks is None else chunks):
        tb = T["t"][c % 2]
        tk = f"lnt{c % 2}"
        P.op("dve", lambda e, c=c, tb=tb: e.tensor_tensor(out=tb[:, 0:nt], in0=z[:, c, :], in1=acc2[:, 0:nt], op=ALU.mult),
             reads=[zk[c], "acc2"], writes=[tk])
        if not rms:
            P.op("dve", lambda e, tb=tb: e.tensor_tensor(out=tb[:, 0:nt], in0=tb[:, 0:nt], in1=sq[:, 0:nt], op=ALU.add),
                 reads=[tk, "sq"], writes=[tk])
            P.op("act", lambda e, c=c, tb=tb: e.activation(out=z[:, c, :], in_=tb[:, 0:nt], func=AF.Identity,
                                                          bias=prm[:, bcol + c:bcol + c + 1], scale=prm[:, gcol + c:gcol + c + 1]),
                 reads=[tk, "prm"], writes=[zk[c]])
        else:
            P.op("act", lambda e, c=c, tb=tb: e.activation(out=z[:, c, :], in_=tb[:, 0:nt], func=AF.Identity,
                                                          bias=prm[:, PC["zero"]:PC["zero"] + 1], scale=prm[:, gcol + c:gcol + c + 1]),
                 reads=[tk, "prm"], writes=[zk[c]])
        after_chunk(c)


FFN_PIECES = [(0, 8), (8, 16), (16, 24), (24, 32), (32, 40), (40, 43)]


def ffn_phase(K, tiles, x_f32, x_bsrc, bsrc_is_f32, w_in, w_out, gcol, bcol, out_f32, out_b16, TT=1024):
    nc, P = K.nc, K.P
    K.uid = getattr(K, "uid", 0) + 1
    with contextlib.ExitStack() as st:
        sb = lambda n, s, d=F32: st.enter_context(nc.sbuf_tensor(f"u{K.uid}_" + n, s, d))
        z = sb("f_z", [128, NDC, TT])
        xb = sb("f_xb", [128, NDC, TT], BF16)
        at = sb("f_at", [128, 8, TT], BF16)
        xs = [sb(f"f_xs{i}", [128, TT]) for i in range(2)]
        sg = [sb(f"f_sg{i}", [128, 512]) for i in range(2)]
        hbs = [sb(f"f_hb{i}", [128, TT], BF16) for i in range(2)]
        K.lnt = {"acc1": sb("f_acc1", [128, TT]), "acc2": sb("f_acc2", [128, TT]), "sq": sb("f_sq", [128, TT]),
                 "t": [sb(f"f_t{i}", [128, TT]) for i in range(2)]}
        nh = TT // 512
        pb = K.pb
        gi = 0
        wo_i = 0
        xs_i = 0
        hb_i = 0
        def load_xb(t0):
            src = x_bsrc[:, t0:t0 + TT].rearrange("(c p) t -> p c t", p=128)
            q = "pool" if bsrc_is_f32 else "sp"
            P.dma(q, "D_fxb", lambda e, src=src: e.dma_start(out=xb[:], in_=src), writes=["f_xb"])
        pending = []
        load_xb(tiles[0][0])
        for ti, (t0, of32, ob16) in enumerate(tiles):
            def after(c, ob16=ob16):
                nonlocal hb_i
                if ob16 is not None:
                    hv = hbs[hb_i % 2]
                    hk = f"f_hb{hb_i % 2}"
                    hb_i += 1
                    P.op("dve", lambda e, hv=hv, c=c: e.tensor_copy(out=hv[:], in_=z[:, c, :]), reads=[f"f_z{c}"], writes=[hk])
                    dst = out_b16[c * 128:(c + 1) * 128, ob16:ob16 + TT]
                    P.dma("sp", "D_" + hk, lambda e, hv=hv, dst=dst: e.dma_start(out=dst, in_=hv[:]), reads=[hk])
            ln_a, ln_s, ln_n = ln_stats_and_norm(K, z, "f_z", TT, gcol, bcol, PC["eps_dn"], after, parts=True)
            for pi, (f0, f1) in enumerate(FFN_PIECES):
                npf = f1 - f0
                fj = f0
                while fj < f1:
                    ns = min(4, f1 - fj)
                    gv, gk = load_w(K, w_in, 0, NDC, fj * 128, ns * 128)
                    uv, uk = load_w(K, w_in, 0, NDC, DFF + fj * 128, ns * 128)
                    for j in range(ns):
                        fl = fj + j - f0
                        for h in range(nh):
                            ba, bb = 2 * (gi % 3), 2 * (gi % 3) + 1
                            gi += 1
                            cs = slice(h * 512, (h + 1) * 512)
                            mm_group(K, pb[ba][:], [(gv[:, kc, j * 128:(j + 1) * 128], xb[:, kc, cs]) for kc in range(NDC)],
                                     reads=[gk, "f_xb"], wkey=f"pb{ba}")
                            mm_group(K, pb[bb][:], [(uv[:, kc, j * 128:(j + 1) * 128], xb[:, kc, cs]) for kc in range(NDC)],
                                     reads=[uk, "f_xb"], wkey=f"pb{bb}")
                            s = sg[gi % 2]
                            sk = f"f_sg{gi % 2}"
                            P.op("act", lambda e, s=s, ba=ba: e.activation(out=s[:], in_=pb[ba][:], func=AF.Silu),
                                 reads=[f"pb{ba}"], writes=[sk])
                            P.op("dve", lambda e, s=s, bb=bb, fl=fl, cs=cs: e.tensor_tensor(out=at[:, fl, cs], in0=s[:], in1=pb[bb][:], op=ALU.mult),
                                 reads=[sk, f"pb{bb}"], writes=[f"f_at{fl}"])
                            if pi == 0 and fj > f0:
                                for _ in range(2):
                                    if pending:
                                        pending.pop(0)()
                    fj += ns
                    if pi == 0 and pending and fj == f0 + ns:
                        pending.pop(0)()
                if pi == 0:
                    while pending:
                        pending.pop(0)()
                if pi == len(FFN_PIECES) - 1 and ti + 1 < len(tiles):
                    load_xb(tiles[ti + 1][0])
                for cs4 in range(4):
                    wv, wk = load_w(K, w_out, f0 * 128, npf, cs4 * 512, 512)
                    for dl in range(4):
                        dc = cs4 * 4 + dl
                        if pi == 0:
                            xv = xs[xs_i % 2]
                            xk = f"f_xs{xs_i % 2}"
                            xs_i += 1
                            xsrc = x_f32[dc * 128:(dc + 1) * 128, t0:t0 + TT]
                            P.dma("sp", "D_" + xk, lambda e, xv=xv, xsrc=xsrc: e.dma_start(out=xv[:], in_=xsrc), writes=[xk])
                        for h in range(nh):
                            bo = 6 + (wo_i % 2)
                            wo_i += 1
                            cs = slice(h * 512, (h + 1) * 512)
                            mm_group(K, pb[bo][:], [(wv[:, k, dl * 128:(dl + 1) * 128], at[:, k, cs]) for k in range(npf)],
                                     reads=[wk] + [f"f_at{k}" for k in range(npf)], wkey=f"pb{bo}")
                            if pi == 0:
                                P.op("dve", lambda e, bo=bo, dc=dc, cs=cs, xv=xv: e.scalar_tensor_tensor(
                                    out=z[:, dc, cs], in0=pb[bo][:], scalar=C_FFN, in1=xv[:, cs], op0=ALU.mult, op1=ALU.add),
                                    reads=[f"pb{bo}", xk], writes=[f"f_z{dc}"])
                            else:
                                P.op("dve", lambda e, bo=bo, dc=dc, cs=cs: e.scalar_tensor_tensor(
                                    out=z[:, dc, cs], in0=pb[bo][:], scalar=C_FFN, in1=z[:, dc, cs], op0=ALU.mult, op1=ALU.add),
                                    reads=[f"pb{bo}", f"f_z{dc}"], writes=[f"f_z{dc}"])
                        if pi == len(FFN_PIECES) - 1:
                            ln_a(dc)

            pending.append(ln_s)
            for c in range(NDC):
                pending.append(lambda c=c, ln_n=ln_n: ln_n(c))

            def tail(of32=of32):
                if of32 is not None:
                    dst = out_f32[:, of32:of32 + TT].rearrange("(c p) t -> p c t", p=128)
                    P.dma("sp", "D_fzo", lambda e, dst=dst: e.dma_start(out=dst, in_=z[:]), reads=[f"f_z{c}" for c in range(NDC)],
                          final=True)
            pending.append(tail)
        while pending:
            pending.pop(0)()
        P.end_phase()


def nb(K, banks):
    b = banks[K.bi % len(banks)]
    K.bi += 1
    return b


def evac_copy(K, out_ap, bank_ap, bkey, wkey):
    K.ev += 1
    if K.ev % 2:
        K.P.op("act", lambda e: e.copy(out=out_ap, in_=bank_ap), reads=[bkey], writes=[wkey])
    else:
        K.P.op("dve", lambda e: e.tensor_copy(out=out_ap, in_=bank_ap), reads=[bkey], writes=[wkey])


def fm_linear(K, W, c0, nch, act, akey, ttiles, evac, nk=NDC, banks=(0, 1, 2, 3), tw=512):
    j = 0
    while j < nch:
        ns = min(4, nch - j)
        wv, wk = load_w(K, W, 0, nk, c0 + j * 128, ns * 128)
        for jj in range(ns):
            for tt in ttiles:
                b = nb(K, banks)
                mm_group(K, K.pb[b][:, 0:tw], [(wv[:, kc, jj * 128:(jj + 1) * 128], act[:, kc, tt * tw:(tt + 1) * tw]) for kc in range(nk)],
                         reads=[wk, akey], wkey=f"pb{b}")
                evac(b, j + jj, tt)
        j += ns


def proj_phase(K, H1B, w_in, QA, KA, VA, CQ, CKV, KPE, G, rope_d, subs=("own", "ctx"), ntt_override=None):
    nc, P = K.nc, K.P
    pb = K.pb
    prm = K.prm
    with contextlib.ExitStack() as st:
        sb = lambda n, s, d=F32: st.enter_context(nc.sbuf_tensor(n, s, d))
        hb = sb("p_hb", [128, NDC, HALF], BF16)
        ql = sb("p_ql", [128, 4, HALF])
        stb = [sb(f"p_stb{i}", [128, HALF], BF16) for i in range(2)]
        stf = [sb(f"p_stf{i}", [128, HALF]) for i in range(2)]
        vst = [sb(f"p_vst{i}", [128, 512], BF16) for i in range(3)]
        cst = [sb(f"p_cst{i}", [128, 512], BF16) for i in range(3)]
        rope = sb("p_rope", [64, 2, HALF])
        rt = [sb(f"p_rt{i}", [64, 512]) for i in range(3)]
        K.lnt = {"acc1": None, "acc2": sb("p_acc2", [128, 512]), "sq": sb("p_sq", [128, 512]),
                 "t": [sb(f"p_t{i}", [128, 512]) for i in range(2)]}
        cnt = {"stb": 0, "stf": 0, "vst": 0, "cst": 0}

        def staged_fm(c0, nch, ttiles, dst, dst_col0, kind):
            def evac(b, j, tt):
                if tt == ttiles[0]:
                    cnt[kind] += 1
                i = cnt[kind] % 2
                if kind == "stb":
                    sv, sk = stb[i], f"p_stb{i}"
                    evac_copy(K, sv[:, tt * 512:(tt + 1) * 512], pb[b][:], f"pb{b}", sk)
                else:
                    sv, sk = stf[i], f"p_stf{i}"
                    gc = PC["gate_b"] + j
                    P.op("act", lambda e, sv=sv, b=b, tt=tt, gc=gc: e.activation(out=sv[:, tt * 512:(tt + 1) * 512], in_=pb[b][:], func=AF.Sigmoid,
                                                                           bias=prm[:, gc:gc + 1], scale=1.0),
                         reads=[f"pb{b}", "prm"], writes=[sk])
                if tt == ttiles[-1]:
                    t0, t1 = ttiles[0] * 512, (ttiles[-1] + 1) * 512
                    d = dst[j * 128:(j + 1) * 128, dst_col0:dst_col0 + (t1 - t0)]
                    P.dma("sp", "D_" + sk, lambda e, sv=sv, d=d, t0=t0, t1=t1: e.dma_start(out=d, in_=sv[:, t0:t1]), reads=[sk])
            fm_linear(K, w_in, c0, nch, hb, "p_hb", ttiles, evac)

        def latent(c0, ttiles, gcol, dst, dst_col0, with_rope):
            def evac(b, j, tt):
                evac_copy(K, ql[:, j, tt * 512:(tt + 1) * 512], pb[b][:], f"pb{b}", f"p_ql{j}_{tt}")
            fm_linear(K, w_in, c0, 4, hb, "p_hb", ttiles, evac)
            if with_rope:
                i = K.ring_i % K.NRING
                K.ring_i += 1
                slot = K.ring[i]
                rv = slot[:, 0:NDC * 128].rearrange("p (k f) -> p k f", k=NDC)
                kc0 = c0 + 512
                for (o, s0, n) in [(0, kc0, 64), (64, kc0 + 32, 32), (96, kc0, 32)]:
                    srcw = w_in[:, s0:s0 + n].rearrange("(k p) f -> p k f", p=128)
                    P.dma("pool", f"D_wr{i}", lambda e, o=o, n=n, srcw=srcw: e.dma_start(out=rv[:, :, o:o + n], in_=srcw), writes=[f"wr{i}"])
            for tt in ttiles:
                zv = ql[:, :, tt * 512:(tt + 1) * 512]

                def after(c, tt=tt):
                    cnt["cst"] += 1
                    ci = cnt["cst"] % 3
                    P.op("dve", lambda e, c=c, tt=tt, ci=ci: e.tensor_copy(out=cst[ci][:], in_=ql[:, c, tt * 512:(tt + 1) * 512]),
                         reads=[f"p_ql{c}_{tt}"], writes=[f"p_cst{ci}"])
                    d = dst[c * 128:(c + 1) * 128, dst_col0 + tt * 512:dst_col0 + (tt + 1) * 512]
                    P.dma("sp", f"D_p_cst{ci}", lambda e, d=d, ci=ci: e.dma_start(out=d, in_=cst[ci][:]), reads=[f"p_cst{ci}"])
                ln_rms_tile(K, zv, [f"p_ql{c}_{tt}" for c in range(4)], gcol, after)
                if with_rope:
                    bx, bs = nb(K, (4, 5, 6, 7)), nb(K, (4, 5, 6, 7))
                    cs = slice(tt * 512, (tt + 1) * 512)
                    mm_group(K, pb[bx][0:64, :], [(rv[:, kc, 0:64], hb[:, kc, cs]) for kc in range(NDC)], reads=[f"wr{i}", "p_hb"], wkey=f"pb{bx}")
                    mm_group(K, pb[bs][0:64, :], [(rv[:, kc, 64:128], hb[:, kc, cs]) for kc in range(NDC)], reads=[f"wr{i}", "p_hb"], wkey=f"pb{bs}")
                    cnt["vst"] += 1
                    r1, r2 = rt[0], rt[1]
                    P.op("dve", lambda e, bx=bx, cs=cs: e.tensor_tensor(out=r1[:], in0=pb[bx][0:64, :], in1=rope[:, 0, cs], op=ALU.mult),
                         reads=[f"pb{bx}", "p_rope"], writes=["p_rt0"])
                    P.op("dve", lambda e, bs=bs, cs=cs: e.tensor_tensor(out=r2[:], in0=pb[bs][0:64, :], in1=rope[:, 1, cs], op=ALU.mult),
                         reads=[f"pb{bs}", "p_rope"], writes=["p_rt1"])
                    vi = cnt["vst"] % 3
                    P.op("dve", lambda e, vi=vi: e.tensor_tensor(out=vst[vi][0:64, :], in0=r1[:], in1=r2[:], op=ALU.add),
                         reads=["p_rt0", "p_rt1"], writes=[f"p_vst{vi}"])
                    d = KPE[:, dst_col0 + tt * 512:dst_col0 + (tt + 1) * 512]
                    P.dma("sp", f"D_p_vst{vi}", lambda e, d=d, vi=vi: e.dma_start(out=d, in_=vst[vi][0:64, :]), reads=[f"p_vst{vi}"])

        def v_tm(tblocks, dst_row0):
            for s in range(2):
                wv, wk = load_w(K, w_in, 0, NDC, 2048 + s * 512, 512)
                for bi_, tb in enumerate(tblocks):
                    b = nb(K, (0, 1, 2, 3))
                    mm_group(K, pb[b][:], [(hb[:, kc, tb * 128:(tb + 1) * 128], wv[:, kc, :]) for kc in range(NDC)], reads=[wk, "p_hb"], wkey=f"pb{b}")
                    cnt["vst"] += 1
                    vi = cnt["vst"] % 3
                    evac_copy(K, vst[vi][:], pb[b][:], f"pb{b}", f"p_vst{vi}")
                    d = VA[dst_row0 + bi_ * 128:dst_row0 + (bi_ + 1) * 128, s * 512:(s + 1) * 512]
                    P.dma("sp", f"D_p_vst{vi}", lambda e, d=d, vi=vi: e.dma_start(out=d, in_=vst[vi][:]), reads=[f"p_vst{vi}"])

        for sub in subs:
            col0 = HALF if sub == "own" else 0
            src = H1B[:, col0:col0 + HALF].rearrange("(c p) t -> p c t", p=128)
            for hh in range(2):
                P.dma("sp", f"D_phb{hh}", lambda e, hh=hh, src=src: e.dma_start(out=hb[:, hh * 8:(hh + 1) * 8, :], in_=src[:, hh * 8:(hh + 1) * 8, :]), writes=["p_hb"])
            P.dma("sp", "D_prope", lambda e, col0=col0: e.dma_start(out=rope[:], in_=rope_d[:, :, col0:col0 + HALF]), writes=["p_rope"])
            tts = list(range(4)) if ntt_override is None else ntt_override
            if sub == "own":
                staged_fm(0, 8, tts, QA, 0, "stb")
                staged_fm(1024, 8, tts, KA, 512, "stb")
                v_tm([tb for tb in range(16) if tb // 4 in tts], 512)
                latent(3072, tts, PC["qan"], CQ, 0, False)
                latent(3584, tts, PC["kvan"], CKV, HALF, True)
                staged_fm(4160, 32, tts, G, 0, "stf")
            else:
                latent(3584, tts, PC["kvan"], CKV, 0, True)
                if 3 in tts:
                    def evac_k(b, j, tt):
                        if True:
                            cnt["stb"] += 1
                        i = cnt["stb"] % 2
                        evac_copy(K, stb[i][:, 0:512], pb[b][:], f"pb{b}", f"p_stb{i}")
                        d = KA[j * 128:(j + 1) * 128, 0:512]
                        P.dma("sp", f"D_p_stb{i}", lambda e, d=d, i=i: e.dma_start(out=d, in_=stb[i][:, 0:512]), reads=[f"p_stb{i}"])
                    fm_linear(K, w_in, 1024, 8, hb, "p_hb", [3], evac_k)
                    v_tm([12, 13, 14, 15], 0)
        P.end_phase()


def ln_rms_tile(K, zv, zkeys, gcol, after_chunk, dtot=512):
    P = K.P
    T = K.lnt
    acc2, sq = T["acc2"], T["sq"]
    prm = K.prm
    pb = K.pb
    n = len(zkeys)
    P.op("act", lambda e: e.activation(out=acc2[:], in_=zv[:, 0, :], func=AF.Square), reads=[zkeys[0]], writes=["acc2"])
    for c in range(1, n):
        P.op("act", lambda e, c=c: e.activation(out=sq[:], in_=zv[:, c, :], func=AF.Square), reads=[zkeys[c]], writes=["sq"])
        P.op("dve", lambda e: e.tensor_tensor(out=acc2[:], in0=acc2[:], in1=sq[:], op=ALU.add), reads=["sq", "acc2"], writes=["acc2"])
    b2 = nb(K, (4, 5, 6, 7))
    P.op("pe", lambda e: e.matmul(pb[b2][:], K.ones_f[:], acc2[:], start=True, stop=True), reads=["acc2", "ones_f"], writes=[f"pb{b2}"])
    ec = PC["eps_rms"]
    P.op("act", lambda e: e.activation(out=acc2[:], in_=pb[b2][:], func=AF.Sqrt, bias=prm[:, ec:ec + 1], scale=1.0 / dtot),
         reads=[f"pb{b2}", "prm"], writes=["acc2"])
    P.op("dve", lambda e: e.reciprocal(out=acc2[:], in_=acc2[:]), reads=["acc2"], writes=["acc2"])
    for c in range(n):
        tb = T["t"][c % 2]
        tk = f"lnt{c % 2}"
        P.op("dve", lambda e, c=c, tb=tb: e.tensor_tensor(out=tb[:], in0=zv[:, c, :], in1=acc2[:], op=ALU.mult), reads=[zkeys[c], "acc2"], writes=[tk])
        P.op("act", lambda e, c=c, tb=tb: e.activation(out=zv[:, c, :], in_=tb[:], func=AF.Identity, bias=prm[:, PC["zero"]:PC["zero"] + 1],
                                                      scale=prm[:, gcol + c:gcol + c + 1]), reads=[tk, "prm"], writes=[zkeys[c]])
        after_chunk(c)


def attnA_phase(K, QA, KA, VA, RVR, MA_d, oA, heads=range(16), qts=range(4)):
    nc, P = K.nc, K.P
    pb = K.pb
    prm = K.prm
    with contextlib.ExitStack() as st:
        sb = lambda n, s, d=F32: st.enter_context(nc.sbuf_tensor(n, s, d))
        MA = sb("a_MA", [128, 8, 512])
        q2 = [sb(f"a_q{i}", [128, 2, HALF], BF16) for i in range(2)]
        k2 = [sb(f"a_k{i}", [128, 2560], BF16) for i in range(2)]
        vraw = [sb(f"a_vr{i}", [128, 20, 128], BF16) for i in range(2)]
        vaug = [sb(f"a_va{i}", [128, 20, 2, 128], BF16) for i in range(2)]
        hk = [sb(f"a_hk{i}", [128, 512]) for i in range(2)]
        TB = [sb(f"a_TB{i}", [128, 8, 512]) for i in range(1)]
        TBb = [sb(f"a_TBb{i}", [128, 8, 512], BF16) for i in range(2)]
        ident_f = sb("a_identf", [128, 128])
        ident = sb("a_ident", [128, 128], BF16)
        P.dma("sp", "D_aid", lambda e: e.dma_start(out=ident_f[:], in_=K.eye_d), writes=["a_identf"])
        P.op("dve", lambda e: e.tensor_copy(out=ident[:], in_=ident_f[:]), reads=["a_identf"], writes=["a_ident"])
        pt = [sb(f"a_p{i}", [128, 512], BF16) for i in range(3)]
        rec = [sb(f"a_rec{i}", [128, 512]) for i in range(2)]
        P.dma("sp", "D_aMA", lambda e: e.dma_start(out=MA[:], in_=MA_d), writes=["a_MA"])
        for i in range(2):
            P.op("pool", lambda e, i=i: e.memset(vaug[i][:], 1.0), writes=[f"a_va{i}"])
            P.op("pool", lambda e, i=i: e.memset(q2[i][:], 0.0), writes=[f"a_q{i}"])
        ci = {"hk": 0, "t": 0, "p": 0, "rec": 0, "o": 0}
        pairs = sorted(set(h // 2 for h in heads))
        for jn, j in enumerate(pairs):
            qv, kv, vr, va = q2[jn % 2], k2[jn % 2], vraw[jn % 2], vaug[jn % 2]
            qk, kk, vrk, vak = f"a_q{jn % 2}", f"a_k{jn % 2}", f"a_vr{jn % 2}", f"a_va{jn % 2}"
            P.dma("sp", "D_" + qk, lambda e, qv=qv, j=j: e.dma_start(out=qv[0:64, 0, :], in_=QA[j * 128:j * 128 + 64, :]), writes=[qk])
            P.dma("sp", "D_" + qk, lambda e, qv=qv, j=j: e.dma_start(out=qv[64:128, 1, :], in_=QA[j * 128 + 64:(j + 1) * 128, :]), writes=[qk])
            P.dma("sp", "D_" + kk, lambda e, kv=kv, j=j: e.dma_start(out=kv[:], in_=KA[j * 128:(j + 1) * 128, :]), writes=[kk])
            vsrc = VA[:, j * 128:(j + 1) * 128].rearrange("(b p) f -> p b f", p=128)
            P.dma("sp", "D_" + vrk, lambda e, vr=vr, vsrc=vsrc: e.dma_start(out=vr[:], in_=vsrc), writes=[vrk])
            P.op("pool", lambda e, vr=vr, va=va: e.tensor_copy(out=va[:, :, 0, 0:64], in_=vr[:, :, 0:64]), reads=[vrk], writes=[vak])
            P.op("pool", lambda e, vr=vr, va=va: e.tensor_copy(out=va[:, :, 1, 64:128], in_=vr[:, :, 64:128]), reads=[vrk], writes=[vak])
            for e_ in range(2):
                h = 2 * j + e_
                if h not in heads:
                    continue
                tbv = TB[0]
                tbk = "a_TB0"
                tbb = TBb[h % 2]
                tbbk = f"a_TBb{h % 2}"
                for b in range(8):
                    ci["hk"] += 1
                    hv = hk[ci["hk"] % 2]
                    hkk = f"a_hk{ci['hk'] % 2}"
                    hsrc = bass.AP(RVR, h * 1536 + 128 * b, [[1, 128], [1, 512]])
                    P.dma("sp", "D_" + hkk, lambda e, hv=hv, hsrc=hsrc: e.dma_start(out=hv[:], in_=hsrc), writes=[hkk])
                    rev = bass.AP(hv, 511, [[hv[:].ap[0][0], 128], [-1, 512]])
                    P.op("dve", lambda e, rev=rev, b=b, tbv=tbv: e.tensor_tensor(out=tbv[:, b, :], in0=rev, in1=MA[:, b, :], op=ALU.add),
                         reads=[hkk, "a_MA"], writes=[tbk + f"_{b}"])
                    P.op("dve", lambda e, b=b, tbv=tbv, tbb=tbb: e.tensor_scalar(out=tbb[:, b, :], in0=tbv[:, b, :], scalar1=8.0, scalar2=None, op0=ALU.mult),
                         reads=[tbk + f"_{b}"], writes=[tbbk + f"_{b}"])
                ps_, pe_ = e_ * 64, (e_ + 1) * 64
                ss_, se_ = (1 - e_) * 64, (2 - e_) * 64
                LOOK = 2
                bo_of = {}
                for qt in qts:
                    ci["o"] += 1
                    bo_of[qt] = 4 + ci["o"] % 2
                pend = []

                def emit_SE(qt, b):
                    kblk = 4 * qt + b
                    bsc = nb(K, (0, 1, 2, 3))
                    P.op("pe", lambda e, bsc=bsc, kblk=kblk, qt=qt, kv=kv, qv=qv, e_=e_: e.matmul(
                        pb[bsc][:], kv[:, kblk * 128:(kblk + 1) * 128], qv[:, e_, qt * 512:(qt + 1) * 512], start=True, stop=False),
                        reads=[kk, qk], writes=[f"pb{bsc}"], inc=False)
                    P.op("pe", lambda e, bsc=bsc, b=b, tbb=tbb: e.matmul(pb[bsc][:], ident[:], tbb[:, b, :], start=False, stop=True),
                         reads=["a_ident", tbbk + f"_{b}"], writes=[f"pb{bsc}"])
                    ci["p"] += 1
                    pv = pt[ci["p"] % 3]
                    pk = f"a_p{ci['p'] % 3}"
                    bc = PC["oth"] if (qt == 0 and b < 4) else PC["zero"]
                    P.op("act", lambda e, pv=pv, bsc=bsc, bc=bc: e.activation(out=pv[:], in_=pb[bsc][:], func=AF.Exp, bias=prm[:, bc:bc + 1], scale=0.125),
                         reads=[f"pb{bsc}", "prm"], writes=[pk])
                    return (qt, b, kblk, pv, pk)

                def emit_V(item):
                    qt, b, kblk, pv, pk = item
                    bo = bo_of[qt]
                    P.op("pe", lambda e, bo=bo, kblk=kblk, pv=pv, b=b, va=va, e_=e_: e.matmul(
                        pb[bo][:], va[:, kblk, e_, :], pv[:], start=(b == 0), stop=(b == 7)),
                        reads=[vak, pk], writes=[f"pb{bo}"])
                    if b == 7:
                        ci["rec"] += 1
                        rv = rec[ci["rec"] % 2]
                        rk = f"a_rec{ci['rec'] % 2}"
                        P.op("dve", lambda e, rv=rv, bo=bo, ss_=ss_, se_=se_: e.reciprocal(out=rv[ss_:se_, :], in_=pb[bo][ss_:se_, :]),
                             reads=[f"pb{bo}"], writes=[rk])
                        P.op("dve", lambda e, rv=rv, bo=bo, qt=qt, ps_=ps_, pe_=pe_, ss_=ss_, se_=se_, j=j: e.tensor_tensor(
                            out=oA[ps_:pe_, j, qt * 512:(qt + 1) * 512], in0=pb[bo][ps_:pe_, :], in1=rv[ss_:se_, :], op=ALU.mult),
                            reads=[f"pb{bo}", rk], writes=[f"oA_{h}_{qt}"])

                for qt in qts:
                    for b in range(8):
                        pend.append(emit_SE(qt, b))
                        if len(pend) > LOOK:
                            emit_V(pend.pop(0))
                while pend:
                    emit_V(pend.pop(0))
        P.end_phase()


MLA_SCALE = 192.0 ** -0.5


def mla_phase(K, CQ, CKV, KPE, w_q_b, w_kv_b, rope_d, MB_d, OB, heads=range(16), qts=range(4)):
    nc, P = K.nc, K.P
    pb = K.pb
    prm = K.prm
    with contextlib.ExitStack() as st:
        sb = lambda n, s, d=F32: st.enter_context(nc.sbuf_tensor(n, s, d))
        cq = sb("m_cq", [128, 4, HALF], BF16)
        ckv = sb("m_ckv", [128, 4, SEQ], BF16)
        kpe = sb("m_kpe", [64, SEQ], BF16)
        ropeq = sb("m_rope", [64, 2, HALF])
        MB = sb("m_MB", [128, 4, 512])
        qn = [sb(f"m_qn{i}", [128, HALF], BF16) for i in range(1)]
        qr = [sb(f"m_qr{i}", [64, HALF], BF16) for i in range(1)]
        kn = [sb(f"m_kn{i}", [128, SEQ], BF16) for i in range(1)]
        vh = [sb(f"m_vh{i}", [128, 32, 128], BF16) for i in range(1)]
        rt = [sb(f"m_rt{i}", [64, 512]) for i in range(2)]
        tt_ = [sb(f"m_t{i}", [128, 512]) for i in range(1)]
        pt = [sb(f"m_p{i}", [128, 512], BF16) for i in range(4)]
        rec = [sb(f"m_rec{i}", [128, 512]) for i in range(1)]
        ost = [sb(f"m_o{i}", [128, 512], BF16) for i in range(2)]
        P.dma("sp", "D_mcq", lambda e: e.dma_start(out=cq[:], in_=CQ.rearrange("(c p) t -> p c t", p=128)), writes=["m_cq"])
        P.dma("sp", "D_mckv", lambda e: e.dma_start(out=ckv[:], in_=CKV.rearrange("(c p) t -> p c t", p=128)), writes=["m_ckv"])
        P.dma("sp", "D_mkpe", lambda e: e.dma_start(out=kpe[:], in_=KPE), writes=["m_kpe"])
        P.dma("sp", "D_mrope", lambda e: e.dma_start(out=ropeq[:], in_=rope_d[:, :, HALF:SEQ]), writes=["m_rope"])
        P.dma("sp", "D_mMB", lambda e: e.dma_start(out=MB[:], in_=MB_d), writes=["m_MB"])
        ci = {"t": 0, "p": 0, "rec": 0, "o": 0, "os": 0}
        for hn, h in enumerate(heads):
            i2 = 0
            qnv, qrv, knv, vhv = qn[i2], qr[i2], kn[i2], vh[i2]
            qnk, qrk, knk, vhk = f"m_qn{i2}", f"m_qr{i2}", f"m_kn{i2}", f"m_vh{i2}"
            i = K.ring_i % K.NRING
            K.ring_i += 1
            wq = K.ring[i][:, 0:4 * 256].rearrange("p (k f) -> p k f", k=4)
            wqk = f"wr{i}"
            for (o, s0, n) in [(0, h * 192, 192), (192, h * 192 + 160, 32), (224, h * 192 + 128, 32)]:
                srcw = w_q_b[:, s0:s0 + n].rearrange("(k p) f -> p k f", p=128)
                P.dma("pool", f"D_wr{i}", lambda e, o=o, n=n, srcw=srcw, wq=wq: e.dma_start(out=wq[:, :, o:o + n], in_=srcw), writes=[wqk])
            wkv, wkvk = load_w(K, w_kv_b, 0, 4, h * 256, 256)
            for tt in range(4):
                cs = slice(tt * 512, (tt + 1) * 512)
                b = nb(K, (0, 1, 2, 3))
                mm_group(K, pb[b][:], [(wq[:, kc, 0:128], cq[:, kc, cs]) for kc in range(4)], reads=[wqk, "m_cq"], wkey=f"pb{b}")
                evac_copy(K, qnv[:, cs], pb[b][:], f"pb{b}", qnk)
                bx, bs = nb(K, (0, 1, 2, 3)), nb(K, (0, 1, 2, 3))
                mm_group(K, pb[bx][0:64, :], [(wq[:, kc, 128:192], cq[:, kc, cs]) for kc in range(4)], reads=[wqk, "m_cq"], wkey=f"pb{bx}")
                mm_group(K, pb[bs][0:64, :], [(wq[:, kc, 192:256], cq[:, kc, cs]) for kc in range(4)], reads=[wqk, "m_cq"], wkey=f"pb{bs}")
                P.op("dve", lambda e, bx=bx, cs=cs: e.tensor_tensor(out=rt[0][:], in0=pb[bx][0:64, :], in1=ropeq[:, 0, cs], op=ALU.mult),
                     reads=[f"pb{bx}", "m_rope"], writes=["m_rt0"])
                P.op("dve", lambda e, bs=bs, cs=cs: e.tensor_tensor(out=rt[1][:], in0=pb[bs][0:64, :], in1=ropeq[:, 1, cs], op=ALU.mult),
                     reads=[f"pb{bs}", "m_rope"], writes=["m_rt1"])
                P.op("pool", lambda e, qrv=qrv, cs=cs: e.tensor_tensor(out=qrv[:, cs], in0=rt[0][:], in1=rt[1][:], op=ALU.add),
                     reads=["m_rt0", "m_rt1"], writes=[qrk])
            for tt in range(8):
                cs = slice(tt * 512, (tt + 1) * 512)
                b = nb(K, (0, 1, 2, 3))
                mm_group(K, pb[b][:], [(wkv[:, kc, 0:128], ckv[:, kc, cs]) for kc in range(4)], reads=[wkvk, "m_ckv"], wkey=f"pb{b}")
                evac_copy(K, knv[:, cs], pb[b][:], f"pb{b}", knk)
            for g in range(8):
                b = nb(K, (0, 1, 2, 3))
                for bl in range(4):
                    blk = 4 * g + bl
                    mm_group(K, pb[b][:, bl * 128:(bl + 1) * 128], [(ckv[:, kc, blk * 128:(blk + 1) * 128], wkv[:, kc, 128:256]) for kc in range(4)],
                             reads=[wkvk, "m_ckv"], wkey=f"pb{b}")
                evac_copy(K, vhv[:, 4 * g:4 * g + 4, :], pb[b][:].rearrange("p (a f) -> p a f", a=4), f"pb{b}", vhk)
            LOOK = 2
            bank_of = {}
            for qt in qts:
                ci["o"] += 1
                bank_of[qt] = (4 + ci["o"] % 2, 6 + ci["o"] % 2)
            pend = []

            def emit_SE(qt, bi_, kb, nblk):
                qs = slice(qt * 512, (qt + 1) * 512)
                ks = slice(kb * 128, (kb + 1) * 128)
                bsc = nb(K, (0, 1, 2, 3))
                P.op("pe", lambda e, bsc=bsc, ks=ks, qs=qs: e.matmul(pb[bsc][:], knv[:, ks], qnv[:, qs], start=True, stop=False),
                     reads=[knk, qnk], writes=[f"pb{bsc}"], inc=False)
                P.op("pe", lambda e, bsc=bsc, ks=ks, qs=qs: e.matmul(pb[bsc][:], kpe[0:64, ks], qrv[0:64, qs], start=False, stop=True),
                     reads=["m_kpe", qrk], writes=[f"pb{bsc}"])
                ci["p"] += 1
                pv = pt[ci["p"] % 4]
                pk = f"m_p{ci['p'] % 4}"
                if kb < 16:
                    bc = PC["oth"]
                    P.op("act", lambda e, pv=pv, bsc=bsc, bc=bc: e.activation(out=pv[:], in_=pb[bsc][:], func=AF.Exp, bias=prm[:, bc:bc + 1], scale=MLA_SCALE),
                         reads=[f"pb{bsc}", "prm"], writes=[pk])
                elif kb - 16 >= 4 * qt:
                    jd = kb - 16 - 4 * qt
                    ci["t"] += 1
                    tv = tt_[0]
                    tk = "m_t0"
                    P.op("dve", lambda e, tv=tv, bsc=bsc, jd=jd: e.scalar_tensor_tensor(out=tv[:], in0=pb[bsc][:], scalar=MLA_SCALE, in1=MB[:, jd, :], op0=ALU.mult, op1=ALU.add),
                         reads=[f"pb{bsc}", "m_MB"], writes=[tk])
                    bc = PC["zero"]
                    P.op("act", lambda e, pv=pv, tv=tv, bc=bc: e.activation(out=pv[:], in_=tv[:], func=AF.Exp, bias=prm[:, bc:bc + 1], scale=1.0),
                         reads=[tk, "prm"], writes=[pk])
                else:
                    bc = PC["zero"]
                    P.op("act", lambda e, pv=pv, bsc=bsc, bc=bc: e.activation(out=pv[:], in_=pb[bsc][:], func=AF.Exp, bias=prm[:, bc:bc + 1], scale=MLA_SCALE),
                         reads=[f"pb{bsc}", "prm"], writes=[pk])
                return (qt, bi_, kb, nblk, pv, pk)

            def emit_V(item):
                qt, bi_, kb, nblk, pv, pk = item
                bo, bsum = bank_of[qt]
                P.op("pe", lambda e, bo=bo, kb=kb, pv=pv, bi_=bi_, nblk=nblk: e.matmul(pb[bo][:], vhv[:, kb, :], pv[:], start=(bi_ == 0), stop=(bi_ == nblk - 1)),
                     reads=[vhk, pk], writes=[f"pb{bo}"])
                P.op("pe", lambda e, bsum=bsum, pv=pv, bi_=bi_, nblk=nblk: e.matmul(pb[bsum][:], K.ones_b[:], pv[:], start=(bi_ == 0), stop=(bi_ == nblk - 1)),
                     reads=["ones_b", pk], writes=[f"pb{bsum}"])
                if bi_ == nblk - 1:
                    ci["rec"] += 1
                    rv = rec[0]
                    rk = "m_rec0"
                    P.op("dve", lambda e, rv=rv, bsum=bsum: e.reciprocal(out=rv[:], in_=pb[bsum][:]), reads=[f"pb{bsum}"], writes=[rk])
                    ci["os"] += 1
                    ov = ost[ci["os"] % 2]
                    ok_ = f"m_o{ci['os'] % 2}"
                    P.op("dve", lambda e, ov=ov, bo=bo, rv=rv: e.tensor_tensor(out=ov[:], in0=pb[bo][:], in1=rv[:], op=ALU.mult),
                         reads=[f"pb{bo}", f"pb{bsum}", rk], writes=[ok_])
                    d = OB[h * 128:(h + 1) * 128, qt * 512:(qt + 1) * 512]
                    P.dma("sp", "D_" + ok_, lambda e, d=d, ov=ov: e.dma_start(out=d, in_=ov[:]), reads=[ok_])

            for qt in qts:
                blocks = list(range(16)) + [16 + kb for kb in range(4 * qt + 4)]
                for bi_, kb in enumerate(blocks):
                    pend.append(emit_SE(qt, bi_, kb, len(blocks)))
                    if len(pend) > LOOK:
                        emit_V(pend.pop(0))
            while pend:
                emit_V(pend.pop(0))
        P.end_phase()


def mixer_phase(K, oA, OB, G, H1, w_o_a, w_o_b, w_out, H2, H2B, tts=range(4)):
    nc, P = K.nc, K.P
    pb = K.pb
    with contextlib.ExitStack() as st:
        sb = lambda n, s, d=F32: st.enter_context(nc.sbuf_tensor(n, s, d))
        ob = sb("x_ob", [128, NDC, 512], BF16)
        m = sb("x_m", [128, NDC, 512], BF16)
        z = sb("x_z", [128, NDC, 512])
        gst = [sb(f"x_g{i}", [128, 2, 512]) for i in range(2)]
        hst = [sb(f"x_h{i}", [128, 512]) for i in range(2)]
        ta = [sb(f"x_ta{i}", [128, 512]) for i in range(2)]
        tb_ = [sb(f"x_tb{i}", [128, 512]) for i in range(2)]
        hbs = [sb(f"x_hb{i}", [128, 512], BF16) for i in range(2)]
        K.lnt = {"acc1": sb("x_acc1", [128, 512]), "acc2": sb("x_acc2", [128, 512]), "sq": sb("x_sq", [128, 512]),
                 "t": [sb(f"x_t{i}", [128, 512]) for i in range(2)]}
        ci = {"g": 0, "h": 0, "t": 0, "hb": 0}
        for tt in tts:
            cs = slice(tt * 512, (tt + 1) * 512)
            P.dma("sp", "D_xob", lambda e, cs=cs: e.dma_start(out=ob[:], in_=OB[:, cs].rearrange("(c p) t -> p c t", p=128)), writes=["x_ob"])
            for sl in range(4):
                woa, woak = load_w(K, w_o_a, 0, 8, sl * 512, 512)
                wob, wobk = load_w(K, w_o_b, 0, NDC, sl * 512, 512)
                for dl in range(4):
                    c = sl * 4 + dl
                    ci["g"] += 1
                    gv = gst[ci["g"] % 2]
                    gk = f"x_g{ci['g'] % 2}"
                    gsrc = G[:, cs].rearrange("(a c p) t -> c p a t", a=2, p=128)[c]
                    P.dma("sp", "D_" + gk, lambda e, gv=gv, gsrc=gsrc: e.dma_start(out=gv[:], in_=gsrc), writes=[gk])
                    ba, bb = nb(K, (0, 1, 2, 3)), nb(K, (0, 1, 2, 3))
                    mm_group(K, pb[ba][:], [(woa[:, kc, dl * 128:(dl + 1) * 128], oA[:, kc, cs]) for kc in range(8)], reads=[woak, "oA"], wkey=f"pb{ba}")
                    mm_group(K, pb[bb][:], [(wob[:, kc, dl * 128:(dl + 1) * 128], ob[:, kc, :]) for kc in range(NDC)], reads=[wobk, "x_ob"], wkey=f"pb{bb}")
                    ci["t"] += 1
                    tav, tbv = ta[ci["t"] % 2], tb_[ci["t"] % 2]
                    tak, tbk = f"x_ta{ci['t'] % 2}", f"x_tb{ci['t'] % 2}"
                    P.op("dve", lambda e, tav=tav, ba=ba, gv=gv: e.tensor_tensor(out=tav[:], in0=pb[ba][:], in1=gv[:, 0, :], op=ALU.mult),
                         reads=[f"pb{ba}", gk], writes=[tak])
                    P.op("dve", lambda e, tbv=tbv, bb=bb, gv=gv: e.tensor_tensor(out=tbv[:], in0=pb[bb][:], in1=gv[:, 1, :], op=ALU.mult),
                         reads=[f"pb{bb}", gk], writes=[tbk])
                    P.op("dve", lambda e, tav=tav, tbv=tbv, c=c: e.tensor_tensor(out=m[:, c, :], in0=tav[:], in1=tbv[:], op=ALU.add),
                         reads=[tak, tbk], writes=[f"x_m{c}"])
            for sl in range(4):
                wo, wok = load_w(K, w_out, 0, NDC, sl * 512, 512)
                for dl in range(4):
                    c = sl * 4 + dl
                    ci["h"] += 1
                    hv = hst[ci["h"] % 2]
                    hk_ = f"x_h{ci['h'] % 2}"
                    P.dma("sp", "D_" + hk_, lambda e, hv=hv, c=c, cs=cs: e.dma_start(out=hv[:], in_=H1[c * 128:(c + 1) * 128, cs]), writes=[hk_])
                    b = nb(K, (4, 5, 6, 7))
                    mm_group(K, pb[b][:], [(wo[:, kc, dl * 128:(dl + 1) * 128], m[:, kc, :]) for kc in range(NDC)],
                             reads=[wok] + [f"x_m{kc}" for kc in range(NDC)], wkey=f"pb{b}")
                    P.op("dve", lambda e, b=b, c=c, hv=hv: e.scalar_tensor_tensor(out=z[:, c, :], in0=pb[b][:], scalar=C_ONE, in1=hv[:], op0=ALU.mult, op1=ALU.add),
                         reads=[f"pb{b}", hk_], writes=[f"x_z{c}"])

            def after(c, cs=cs):
                ci["hb"] += 1
                hv2 = hbs[ci["hb"] % 2]
                hk2 = f"x_hb{ci['hb'] % 2}"
                P.op("dve", lambda e, hv2=hv2, c=c: e.tensor_copy(out=hv2[:], in_=z[:, c, :]), reads=[f"x_z{c}"], writes=[hk2])
                d = H2B[c * 128:(c + 1) * 128, cs]
                P.dma("sp", "D_" + hk2, lambda e, hv2=hv2, d=d: e.dma_start(out=d, in_=hv2[:]), reads=[hk2])
            ln_stats_and_norm(K, z, "x_z", 512, PC["ln2_g"], PC["ln2_b"], PC["eps_dn"], after)
            d = H2[:, cs].rearrange("(c p) t -> p c t", p=128)
            P.dma("sp", "D_xzo", lambda e, d=d: e.dma_start(out=d, in_=z[:]), reads=[f"x_z{c}" for c in range(NDC)])
        P.end_phase()


def memkv_phase(K, memT, w_xkv, KX, VX):
    nc, P = K.nc, K.P
    pb = K.pb
    with contextlib.ExitStack() as st:
        sb = lambda n, s, d=F32: st.enter_context(nc.sbuf_tensor(n, s, d))
        z = sb("k_z", [128, NDC, 256])
        mb = sb("k_mb", [128, NDC, 256], BF16)
        K.lnt = {"acc1": sb("k_acc1", [128, 256]), "acc2": sb("k_acc2", [128, 256]), "sq": sb("k_sq", [128, 256]),
                 "t": [sb(f"k_t{i}", [128, 256]) for i in range(2)]}
        P.dma("sp", "D_kz", lambda e: e.dma_start(out=z[:], in_=memT.rearrange("(c p) t -> p c t", p=128)), writes=[f"k_z{c}" for c in range(NDC)])

        def after(c):
            P.op("dve", lambda e, c=c: e.tensor_copy(out=mb[:, c, :], in_=z[:, c, :]), reads=[f"k_z{c}"], writes=["k_mb"])
        ln_stats_and_norm(K, z, "k_z", 256, PC["mln_g"], PC["mln_b"], PC["eps_ln"], after)
        for sl in range(4):
            wv, wk = load_w(K, w_xkv, 0, NDC, sl * 512, 512)
            for jj in range(4):
                b = nb(K, (0, 1, 2, 3))
                mm_group(K, pb[b][:, 0:256], [(wv[:, kc, jj * 128:(jj + 1) * 128], mb[:, kc, :]) for kc in range(NDC)], reads=[wk, "k_mb"], wkey=f"pb{b}")
                evac_copy(K, KX[:, sl * 4 + jj, :], pb[b][:, 0:256], f"pb{b}", "KX")
        for sl in range(4):
            wv, wk = load_w(K, w_xkv, 0, NDC, D + sl * 512, 512)
            for mbk in range(2):
                b = nb(K, (0, 1, 2, 3))
                mm_group(K, pb[b][:], [(mb[:, kc, mbk * 128:(mbk + 1) * 128], wv[:, kc, :]) for kc in range(NDC)], reads=[wk, "k_mb"], wkey=f"pb{b}")
                evac_copy(K, VX[:, mbk, sl * 512:(sl + 1) * 512], pb[b][:], f"pb{b}", "VX")
        P.end_phase()


X_SCALE = 512.0 ** -0.5


def cross_phase(K, H2, H2B, w_xq, w_xo, KX, VX, H3, H3B, tts=range(4)):
    nc, P = K.nc, K.P
    pb = K.pb
    prm = K.prm
    with contextlib.ExitStack() as st:
        sb = lambda n, s, d=F32: st.enter_context(nc.sbuf_tensor(n, s, d))
        hb = sb("c_hb", [128, NDC, 512], BF16)
        qx = sb("c_qx", [128, NDC, 512], BF16)
        ox = sb("c_ox", [128, NDC, 512], BF16)
        z = sb("c_z", [128, NDC, 512])
        hst = [sb(f"c_h{i}", [128, 512]) for i in range(2)]
        pt = [sb(f"c_p{i}", [128, 2, 512], BF16) for i in range(2)]
        rec = [sb(f"c_rec{i}", [128, 512]) for i in range(2)]
        hbs = [sb(f"c_hbs{i}", [128, 512], BF16) for i in range(2)]
        K.lnt = {"acc1": sb("c_acc1", [128, 512]), "acc2": sb("c_acc2", [128, 512]), "sq": sb("c_sq", [128, 512]),
                 "t": [sb(f"c_t{i}", [128, 512]) for i in range(2)]}
        ci = {"h": 0, "hb": 0}
        zc = PC["zero"]
        for tt in tts:
            cs = slice(tt * 512, (tt + 1) * 512)
            P.dma("sp", "D_chb", lambda e, cs=cs: e.dma_start(out=hb[:], in_=H2B[:, cs].rearrange("(c p) t -> p c t", p=128)), writes=["c_hb"])

            def evq(b, j, t_):
                evac_copy(K, qx[:, j, :], pb[b][:], f"pb{b}", f"c_qx{j}")
            fm_linear(K, w_xq, 0, NDC, hb, "c_hb", [0], evq)
            for hx in range(4):
                pv = pt[hx % 2]
                pk = f"c_p{hx % 2}"
                rv = rec[hx % 2]
                rk = f"c_rec{hx % 2}"
                for mbk in range(2):
                    b = nb(K, (0, 1, 2, 3))
                    mm_group(K, pb[b][:], [(KX[:, hx * 4 + cc, mbk * 128:(mbk + 1) * 128], qx[:, hx * 4 + cc, :]) for cc in range(4)],
                             reads=["KX"] + [f"c_qx{hx * 4 + cc}" for cc in range(4)], wkey=f"pb{b}")
                    P.op("act", lambda e, pv=pv, b=b, mbk=mbk: e.activation(out=pv[:, mbk, :], in_=pb[b][:], func=AF.Exp, bias=prm[:, zc:zc + 1], scale=X_SCALE),
                         reads=[f"pb{b}", "prm"], writes=[pk])
                bs_ = nb(K, (4, 5, 6, 7))
                mm_group(K, pb[bs_][:], [(K.ones_b[:], pv[:, mbk, :]) for mbk in range(2)], reads=["ones_b", pk], wkey=f"pb{bs_}")
                P.op("dve", lambda e, rv=rv, bs_=bs_: e.reciprocal(out=rv[:], in_=pb[bs_][:]), reads=[f"pb{bs_}"], writes=[rk])
                for cc in range(4):
                    b = nb(K, (4, 5, 6, 7))
                    col = hx * 512 + cc * 128
                    mm_group(K, pb[b][:], [(VX[:, mbk, col:col + 128], pv[:, mbk, :]) for mbk in range(2)], reads=["VX", pk], wkey=f"pb{b}")
                    P.op("dve", lambda e, b=b, hx=hx, cc=cc, rv=rv: e.tensor_tensor(out=ox[:, hx * 4 + cc, :], in0=pb[b][:], in1=rv[:], op=ALU.mult),
                         reads=[f"pb{b}", rk], writes=[f"c_ox{hx * 4 + cc}"])
            for sl in range(4):
                wo, wok = load_w(K, w_xo, 0, NDC, sl * 512, 512)
                for dl in range(4):
                    c = sl * 4 + dl
                    ci["h"] += 1
                    hv = hst[ci["h"] % 2]
                    hk_ = f"c_h{ci['h'] % 2}"
                    P.dma("sp", "D_" + hk_, lambda e, hv=hv, c=c, cs=cs: e.dma_start(out=hv[:], in_=H2[c * 128:(c + 1) * 128, cs]), writes=[hk_])
                    b = nb(K, (0, 1, 2, 3))
                    mm_group(K, pb[b][:], [(wo[:, kc, dl * 128:(dl + 1) * 128], ox[:, kc, :]) for kc in range(NDC)],
                             reads=[wok] + [f"c_ox{kc}" for kc in range(NDC)], wkey=f"pb{b}")
                    P.op("dve", lambda e, b=b, c=c, hv=hv: e.scalar_tensor_tensor(out=z[:, c, :], in0=pb[b][:], scalar=C_ONE, in1=hv[:], op0=ALU.mult, op1=ALU.add),
                         reads=[f"pb{b}", hk_], writes=[f"c_z{c}"])

            def after(c, cs=cs):
                ci["hb"] += 1
                hv2 = hbs[ci["hb"] % 2]
                hk2 = f"c_hbs{ci['hb'] % 2}"
                P.op("dve", lambda e, hv2=hv2, c=c: e.tensor_copy(out=hv2[:], in_=z[:, c, :]), reads=[f"c_z{c}"], writes=[hk2])
                d = H3B[c * 128:(c + 1) * 128, cs]
                P.dma("sp", "D_" + hk2, lambda e, hv2=hv2, d=d: e.dma_start(out=d, in_=hv2[:]), reads=[hk2])
            ln_stats_and_norm(K, z, "c_z", 512, PC["ln3_g"], PC["ln3_b"], PC["eps_dn"], after)
            d = H3[:, cs].rearrange("(c p) t -> p c t", p=128)
            P.dma("sp", "D_czo", lambda e, d=d: e.dma_start(out=d, in_=z[:]), reads=[f"c_z{c}" for c in range(NDC)])
        P.end_phase()


def mixer_m_phase(K, oA, OB, G, w_o_a, w_o_b, M):
    nc, P = K.nc, K.P
    pb = K.pb
    with contextlib.ExitStack() as st:
        sb = lambda n, s, d=F32: st.enter_context(nc.sbuf_tensor(n, s, d))
        ob = sb("y_ob", [128, NDC, 1024], BF16)
        gst = [sb(f"y_g{i}", [128, 2, 512]) for i in range(2)]
        ta = [sb(f"y_ta{i}", [128, 512]) for i in range(2)]
        tb_ = [sb(f"y_tb{i}", [128, 512]) for i in range(2)]
        mst = [sb(f"y_m{i}", [128, 512], BF16) for i in range(3)]
        ci = {"g": 0, "t": 0, "m": 0}
        for T in range(2):
            P.dma("sp", "D_yob", lambda e, T=T: e.dma_start(out=ob[:], in_=OB[:, T * 1024:(T + 1) * 1024].rearrange("(c p) t -> p c t", p=128)),
                  writes=["y_ob"])
            for sl in range(4):
                woa, woak = load_w(K, w_o_a, 0, 8, sl * 512, 512)
                wob, wobk = load_w(K, w_o_b, 0, NDC, sl * 512, 512)
                for dl in range(4):
                    c = sl * 4 + dl
                    for hf in range(2):
                        g0 = T * 1024 + hf * 512
                        cs = slice(g0, g0 + 512)
                        ls = slice(hf * 512, (hf + 1) * 512)
                        ci["g"] += 1
                        gv = gst[ci["g"] % 2]
                        gk = f"y_g{ci['g'] % 2}"
                        gsrc = G[:, cs].rearrange("(a c p) t -> c p a t", a=2, p=128)[c]
                        P.dma("sp", "D_" + gk, lambda e, gv=gv, gsrc=gsrc: e.dma_start(out=gv[:], in_=gsrc), writes=[gk])
                        ba, bb = nb(K, (0, 1, 2, 3)), nb(K, (0, 1, 2, 3))
                        mm_group(K, pb[ba][:], [(woa[:, kc, dl * 128:(dl + 1) * 128], oA[:, kc, cs]) for kc in range(8)], reads=[woak, "oA"], wkey=f"pb{ba}")
                        mm_group(K, pb[bb][:], [(wob[:, kc, dl * 128:(dl + 1) * 128], ob[:, kc, ls]) for kc in range(NDC)], reads=[wobk, "y_ob"], wkey=f"pb{bb}")
                        ci["t"] += 1
                        tav, tbv = ta[ci["t"] % 2], tb_[ci["t"] % 2]
                        tak, tbk = f"y_ta{ci['t'] % 2}", f"y_tb{ci['t'] % 2}"
                        P.op("dve", lambda e, tav=tav, ba=ba, gv=gv: e.tensor_tensor(out=tav[:], in0=pb[ba][:], in1=gv[:, 0, :], op=ALU.mult),
                             reads=[f"pb{ba}", gk], writes=[tak])
                        P.op("dve", lambda e, tbv=tbv, bb=bb, gv=gv: e.tensor_tensor(out=tbv[:], in0=pb[bb][:], in1=gv[:, 1, :], op=ALU.mult),
                             reads=[f"pb{bb}", gk], writes=[tbk])
                        ci["m"] += 1
                        mv = mst[ci["m"] % 3]
                        mk = f"y_m{ci['m'] % 3}"
                        P.op("dve", lambda e, tav=tav, tbv=tbv, mv=mv: e.tensor_tensor(out=mv[:], in0=tav[:], in1=tbv[:], op=ALU.add),
                             reads=[tak, tbk], writes=[mk])
                        d = M[c * 128:(c + 1) * 128, cs]
                        P.dma("sp", "D_" + mk, lambda e, mv=mv, d=d: e.dma_start(out=d, in_=mv[:]), reads=[mk])
        P.end_phase()


def out_ln_phase(K, ACT, W, RES, gcol, bcol, OUTF, OUTB):
    nc, P = K.nc, K.P
    pb = K.pb
    K.uid = getattr(K, "uid", 0) + 1
    with contextlib.ExitStack() as st:
        sb = lambda n, s, d=F32: st.enter_context(nc.sbuf_tensor(f"u{K.uid}_" + n, s, d))
        act = sb("o_act", [128, NDC, 1024], BF16)
        z = sb("o_z", [128, NDC, 1024])
        hst = [sb(f"o_h{i}", [128, 512]) for i in range(2)]
        hbs = [sb(f"o_hb{i}", [128, 1024], BF16) for i in range(2)]
        K.lnt = {"acc1": sb("o_acc1", [128, 1024]), "acc2": sb("o_acc2", [128, 1024]), "sq": sb("o_sq", [128, 1024]),
                 "t": [sb(f"o_t{i}", [128, 1024]) for i in range(2)]}
        ci = {"h": 0, "hb": 0}
        for T in range(2):
            P.dma("sp", "D_oact", lambda e, T=T: e.dma_start(out=act[:], in_=ACT[:, T * 1024:(T + 1) * 1024].rearrange("(c p) t -> p c t", p=128)),
                  writes=["o_act"])
            for sl in range(4):
                wo, wok = load_w(K, W, 0, NDC, sl * 512, 512)
                for dl in range(4):
                    c = sl * 4 + dl
                    for hf in range(2):
                        g0 = T * 1024 + hf * 512
                        ls = slice(hf * 512, (hf + 1) * 512)
                        ci["h"] += 1
                        hv = hst[ci["h"] % 2]
                        hk_ = f"o_h{ci['h'] % 2}"
                        P.dma("sp", "D_" + hk_, lambda e, hv=hv, c=c, g0=g0: e.dma_start(out=hv[:], in_=RES[c * 128:(c + 1) * 128, g0:g0 + 512]), writes=[hk_])
                        b = nb(K, (4, 5, 6, 7))
                        mm_group(K, pb[b][:], [(wo[:, kc, dl * 128:(dl + 1) * 128], act[:, kc, ls]) for kc in range(NDC)], reads=[wok, "o_act"], wkey=f"pb{b}")
                        P.op("dve", lambda e, b=b, c=c, hv=hv, ls=ls: e.scalar_tensor_tensor(out=z[:, c, ls], in0=pb[b][:], scalar=C_ONE, in1=hv[:], op0=ALU.mult, op1=ALU.add),
                             reads=[f"pb{b}", hk_], writes=[f"o_z{c}"])

            def after(c, T=T):
                ci["hb"] += 1
                hv2 = hbs[ci["hb"] % 2]
                hk2 = f"o_hb{ci['hb'] % 2}"
                P.op("dve", lambda e, hv2=hv2, c=c: e.tensor_copy(out=hv2[:], in_=z[:, c, :]), reads=[f"o_z{c}"], writes=[hk2])
                d = OUTB[c * 128:(c + 1) * 128, T * 1024:(T + 1) * 1024]
                P.dma("sp", "D_" + hk2, lambda e, hv2=hv2, d=d: e.dma_start(out=d, in_=hv2[:]), reads=[hk2])
            ln_stats_and_norm(K, z, "o_z", 1024, gcol, bcol, PC["eps_dn"], after)
            d = OUTF[:, T * 1024:(T + 1) * 1024].rearrange("(c p) t -> p c t", p=128)
            P.dma("sp", "D_ozo", lambda e, d=d: e.dma_start(out=d, in_=z[:]), reads=[f"o_z{c}" for c in range(NDC)])
        P.end_phase()


def cross_q_phase(K, H2B, w_xq, KX, VX, OX):
    nc, P = K.nc, K.P
    pb = K.pb
    prm = K.prm
    with contextlib.ExitStack() as st:
        sb = lambda n, s, d=F32: st.enter_context(nc.sbuf_tensor(n, s, d))
        hb = sb("d_hb", [128, NDC, 1024], BF16)
        qx = sb("d_qx", [128, NDC, 1024], BF16)
        pt = [sb(f"d_p{i}", [128, 2, 512], BF16) for i in range(2)]
        rec = [sb(f"d_rec{i}", [128, 512]) for i in range(2)]
        ost = [sb(f"d_o{i}", [128, 512], BF16) for i in range(3)]
        ci = {"o": 0, "x": 0}
        zc = PC["zero"]
        for T in range(2):
            P.dma("sp", "D_dhb", lambda e, T=T: e.dma_start(out=hb[:], in_=H2B[:, T * 1024:(T + 1) * 1024].rearrange("(c p) t -> p c t", p=128)),
                  writes=["d_hb"])

            def evq(b, j, t_):
                evac_copy(K, qx[:, j, t_ * 512:(t_ + 1) * 512], pb[b][:], f"pb{b}", f"d_qx{j}_{t_}")
            fm_linear(K, w_xq, 0, NDC, hb, "d_hb", [0, 1], evq)
            for hf in range(2):
                ls = slice(hf * 512, (hf + 1) * 512)
                g0 = T * 1024 + hf * 512
                for hx in range(4):
                    ci["x"] += 1
                    pv = pt[ci["x"] % 2]
                    pk = f"d_p{ci['x'] % 2}"
                    rv = rec[ci["x"] % 2]
                    rk = f"d_rec{ci['x'] % 2}"
                    for mbk in range(2):
                        b = nb(K, (0, 1, 2, 3))
                        mm_group(K, pb[b][:], [(KX[:, hx * 4 + cc, mbk * 128:(mbk + 1) * 128], qx[:, hx * 4 + cc, ls]) for cc in range(4)],
                                 reads=["KX"] + [f"d_qx{hx * 4 + cc}_{hf}" for cc in range(4)], wkey=f"pb{b}")
                        P.op("act", lambda e, pv=pv, b=b, mbk=mbk: e.activation(out=pv[:, mbk, :], in_=pb[b][:], func=AF.Exp, bias=prm[:, zc:zc + 1], scale=X_SCALE),
                             reads=[f"pb{b}", "prm"], writes=[pk])
                    bs_ = nb(K, (4, 5, 6, 7))
                    mm_group(K, pb[bs_][:], [(K.ones_b[:], pv[:, mbk, :]) for mbk in range(2)], reads=["ones_b", pk], wkey=f"pb{bs_}")
                    P.op("dve", lambda e, rv=rv, bs_=bs_: e.reciprocal(out=rv[:], in_=pb[bs_][:]), reads=[f"pb{bs_}"], writes=[rk])
                    for cc in range(4):
                        b = nb(K, (4, 5, 6, 7))
                        col = hx * 512 + cc * 128
                        mm_group(K, pb[b][:], [(VX[:, mbk, col:col + 128], pv[:, mbk, :]) for mbk in range(2)], reads=["VX", pk], wkey=f"pb{b}")
                        ci["o"] += 1
                        ov = ost[ci["o"] % 3]
                        ok_ = f"d_o{ci['o'] % 3}"
                        P.op("dve", lambda e, b=b, ov=ov, rv=rv: e.tensor_tensor(out=ov[:], in0=pb[b][:], in1=rv[:], op=ALU.mult),
                             reads=[f"pb{b}", rk], writes=[ok_])
                        d = OX[(hx * 4 + cc) * 128:(hx * 4 + cc + 1) * 128, g0:g0 + 512]
                        P.dma("sp", "D_" + ok_, lambda e, ov=ov, d=d: e.dma_start(out=d, in_=ov[:]), reads=[ok_])
        P.end_phase()


def build(mode="full", ext=None):
    ext = ext or {}
    ph = set(mode.split("+"))
    full = "full" in ph
    on = lambda p: full or p in ph
    nc = bass.Bass("TRN2", target_bir_lowering=False)
    K = Ctx()
    K.nc = nc
    K.bi = 0
    K.ev = 0
    TH = {}

    def dram(name, shape, dt, kind=None):
        kind = kind or ext.get(name, "Internal")
        t = nc.dram_tensor(name, shape, dt, kind=kind)
        TH[name] = t
        return t.ap()

    IN = "ExternalInput"
    xT = dram("xT", [D, SEQ], F32, IN)
    memT = dram("memT", [D, 256], F32, IN)
    prm_d = dram("prm", [128, NPRM], F32, IN)
    rope_d = dram("rope", [64, 2, SEQ], F32, IN)
    dram("rvr", [16, 1536], F32, IN)
    MA_d = dram("MA", [128, 8, 512], F32, IN)
    MB_d = dram("MB", [128, 4, 512], F32, IN)
    K.eye_d = dram("eye", [128, 128], F32, IN)
    w = {}
    for n, shp in [("ffn1_w_in", [D, 2 * DFF]), ("ffn1_w_out", [DFF, D]), ("w_in", [D, 8256]), ("w_q_b", [512, 3072]),
                   ("w_kv_b", [512, 4096]), ("w_o_a", [1024, D]), ("w_o_b", [D, D]), ("w_out", [D, D]), ("w_xq", [D, D]),
                   ("w_xkv", [D, 2 * D]), ("w_xo", [D, D]), ("ffn2_w_in", [D, 2 * DFF]), ("ffn2_w_out", [DFF, D])]:
        w[n] = dram(n, shp, F32, IN)
    H1 = dram("H1", [D, HALF], F32)
    H1B = dram("H1B", [D, SEQ], BF16)
    QA = dram("QA", [1024, HALF], BF16)
    KA = dram("KA", [1024, 2560], BF16)
    VA = dram("VA", [2560, 1024], BF16)
    CQ = dram("CQ", [512, HALF], BF16)
    CKV = dram("CKV", [512, SEQ], BF16)
    KPE = dram("KPE", [64, SEQ], BF16)
    G = dram("G", [2 * D, HALF], F32)
    OB = dram("OB", [D, HALF], BF16)
    H2 = dram("H2", [D, HALF], F32)
    H2B = dram("H2B", [D, HALF], BF16)
    H3 = dram("H3", [D, HALF], F32)
    H3B = dram("H3B", [D, HALF], BF16)
    OUT = dram("out", [D, HALF], F32, "ExternalOutput")
    OAD = dram("OAD", [1024, HALF], BF16)
    MD = dram("MD", [D, HALF], BF16)
    OXD = dram("OXD", [D, HALF], BF16)

    with contextlib.ExitStack() as st:
        P = Prog(nc, st)
        K.P = P
        sb = lambda n, s, d=F32: st.enter_context(nc.sbuf_tensor(n, s, d))
        K.NRING = 3
        K.ring = [sb(f"wr{i}", [128, 8192], BF16) for i in range(K.NRING)]
        K.ring_i = 0
        K.pb = [st.enter_context(nc.psum_tensor(f"pb{i}", [128, 512], F32)) for i in range(8)]
        K.prm = sb("prm_sb", [128, NPRM])
        K.ones_f = sb("ones_f", [128, 128])
        K.ones_b = sb("ones_b", [128, 128], BF16)
        P.dma("sp", "D_prm", lambda e: e.dma_start(out=K.prm[:], in_=prm_d), writes=["prm"])
        P.op("dve", lambda e: e.memset(K.ones_f[:], 1.0), writes=["ones_f"])
        P.op("dve", lambda e: e.memset(K.ones_b[:], 1.0), writes=["ones_b"])
        P.end_phase()

        if on("ffn1"):
            tiles = ext.get("tiles1", [(0, None, 0), (1024, None, 1024), (2048, 0, 2048), (3072, 1024, 3072)])
            ffn_phase(K, tiles, xT, xT, True, w["ffn1_w_in"], w["ffn1_w_out"], PC["ln1_g"], PC["ln1_b"], H1, H1B)
        if on("proj"):
            proj_phase(K, H1B, w["w_in"], QA, KA, VA, CQ, CKV, KPE, G, rope_d, subs=ext.get("subs", ("own", "ctx")),
                       ntt_override=ext.get("proj_tts"))
        if on("attna") or on("mla") or on("mixer"):
            with contextlib.ExitStack() as st2:
                oA = st2.enter_context(nc.sbuf_tensor("oA", [128, 8, HALF], BF16))
                if on("attna"):
                    attnA_phase(K, QA, KA, VA, TH["rvr"], MA_d, oA, heads=ext.get("a_heads", range(16)), qts=ext.get("a_qts", range(4)))
                    if "OAD" in ext:
                        P.dma("sp", "D_oad", lambda e: e.dma_start(out=OAD.rearrange("(c p) t -> p c t", p=128), in_=oA[:]), final=True)
                        P.end_phase()
                if on("mla"):
                    mla_phase(K, CQ, CKV, KPE, w["w_q_b"], w["w_kv_b"], rope_d, MB_d, OB, heads=ext.get("m_heads", range(16)),
                              qts=ext.get("m_qts", range(4)))
                if on("mixer"):
                    if ext.get("OAD") == "ExternalInput":
                        P.dma("sp", "D_oad", lambda e: e.dma_start(out=oA[:], in_=OAD.rearrange("(c p) t -> p c t", p=128)), writes=["oA"])
                    mixer_m_phase(K, oA, OB, G, w["w_o_a"], w["w_o_b"], MD)
        if on("mixer"):
            out_ln_phase(K, MD, w["w_out"], H1, PC["ln2_g"], PC["ln2_b"], H2, H2B)
        if on("cross"):
            with contextlib.ExitStack() as st2:
                KX = st2.enter_context(nc.sbuf_tensor("KX", [128, NDC, 256], BF16))
                VX = st2.enter_context(nc.sbuf_tensor("VX", [128, 2, D], BF16))
                memkv_phase(K, memT, w["w_xkv"], KX, VX)
                cross_q_phase(K, H2B, w["w_xq"], KX, VX, OXD)
                out_ln_phase(K, OXD, w["w_xo"], H2, PC["ln3_g"], PC["ln3_b"], H3, H3B)
        if on("ffn2"):
            tiles = ext.get("tiles2", [(0, 0, None), (1024, 1024, None)])
            ffn_phase(K, tiles, H3, H3B, False, w["ffn2_w_in"], w["ffn2_w_out"], PC["ln4_g"], PC["ln4_b"], OUT, None)
        fw = {}
        for s, v in P.final_waits:
            fw[s] = max(fw.get(s, 0), v)
        P.ops["sp"].append((list(fw.items()), None, None))
        P.end_phase()
    return nc


def col_layout(v):
    v = np.asarray(v, np.float32).reshape(-1)
    return np.ascontiguousarray(v.reshape(-1, 128).T)


def make_prm(inputs, hi):
    prm = np.zeros((128, NPRM), np.float32)

    def put(name, v):
        c = col_layout(v)
        prm[:, PC[name]:PC[name] + c.shape[1]] = c
    put("ln1_g", inputs["ln_ffn1_g"][0]); put("ln1_b", inputs["ln_ffn1_b"][0])
    put("ln2_g", inputs["ln_mix_g"][0]); put("ln2_b", inputs["ln_mix_b"][0])
    put("ln3_g", inputs["ln_x_g"][0]); put("ln3_b", inputs["ln_x_b"][0])
    put("ln4_g", inputs["ln_ffn2_g"][0]); put("ln4_b", inputs["ln_ffn2_b"][0])
    put("mln_g", inputs["mem_ln_g"][0]); put("mln_b", inputs["mem_ln_b"][0])
    put("gate_b", inputs["gate_bias"][0])
    put("qan", inputs["q_a_norm"][0]); put("kvan", inputs["kv_a_norm"][0])
    prm[:, PC["oth"]] = 0.0 if hi else NEG
    prm[:, PC["eps_dn"]] = 1e-5 / (ALPHA * ALPHA)
    prm[:, PC["eps_ln"]] = 1e-5
    prm[:, PC["eps_rms"]] = 1e-6
    return prm


def make_rope(hi):
    inv = (1.0 / (np.float32(10000.0) ** (np.arange(0, 64, 2, dtype=np.float32) / np.float32(64)))).astype(np.float32)
    own = np.arange(HALF, SEQ) if hi else np.arange(0, HALF)
    pos = np.concatenate([np.arange(0, HALF), own]).astype(np.float32)
    ang = (pos[None, :] * inv[:, None]).astype(np.float32)
    cos, sin = np.cos(ang).astype(np.float32), np.sin(ang).astype(np.float32)
    tab = np.empty((64, 2, SEQ), np.float32)
    tab[0:32, 0] = cos
    tab[32:64, 0] = cos
    tab[0:32, 1] = -sin
    tab[32:64, 1] = sin
    return tab


def make_masks():
    k = np.arange(128)[:, None]
    q = np.arange(512)[None, :]
    MA = np.zeros((128, 8, 512), np.float32)
    for b in range(8):
        diff = 8 + q // 64 - 2 * b - k // 64
        MA[:, b, :] = np.where((diff >= 0) & (diff <= 8), 0.0, NEG)
    MB = np.zeros((128, 4, 512), np.float32)
    for j in range(4):
        MB[:, j, :] = np.where((128 * j + k) // 64 <= q // 64, 0.0, NEG)
    return MA, MB


def make_rvr(rel_bias):
    idx = np.clip(1023 - np.arange(1536), -128, 128) + 128
    return np.ascontiguousarray(np.asarray(rel_bias, np.float32)[:, idx])


WNAMES = ["ffn1_w_in", "ffn1_w_out", "w_in", "w_q_b", "w_kv_b", "w_o_a", "w_o_b", "w_out", "w_xq", "w_xkv", "w_xo",
          "ffn2_w_in", "ffn2_w_out"]


def core_inputs(inputs, c, shared=None):
    b, hf = divmod(c, 2)
    hi = hf == 1
    x = np.asarray(inputs["x"][b], np.float32)
    xT = np.zeros((D, SEQ), np.float32)
    if hi:
        xT[:] = x.T
    else:
        xT[:, HALF:] = x[0:HALF].T
    m = {"xT": xT, "memT": np.ascontiguousarray(np.asarray(inputs["mem"][b], np.float32).T),
         "prm": make_prm(inputs, hi), "rope": make_rope(hi)}
    if shared is None:
        shared = shared_inputs(inputs)
    m.update(shared)
    return m


def shared_inputs(inputs):
    MA, MB = make_masks()
    s = {"MA": MA, "MB": MB, "rvr": make_rvr(inputs["rel_bias"][0]), "eye": np.eye(128, dtype=np.float32)}
    for n in WNAMES:
        s[n] = np.ascontiguousarray(np.asarray(inputs[n][0], np.float32))
    return s


def kernel(**inputs):
    nc = build("full")
    shared = shared_inputs(inputs)
    in_maps = [core_inputs(inputs, c, shared) for c in range(8)]
    res = run_bass_kernel_spmd(nc, in_maps, core_ids=list(range(8)))
    out = np.empty((4, SEQ, D), np.float32)
    for c in range(8):
        b, hf = divmod(c, 2)
        out[b, hf * HALF:(hf + 1) * HALF, :] = np.asarray(res.results[c]["out"], np.float32).T
    return out
```
